# Optimizing a Trainium2 kernel written in Bass

```python
import jax, jax.numpy as jnp
from jax import lax
import numpy as np

D_MODEL = 1024
BATCH = 4
SEQ = 4096
DEPTH = 2
DEC_BATCH = 8
DEC_SEQ = 2048
PAST_LEN = 128

GRID_W = 64
HEAD_DIM = 64
D_FF = 2752
NORM_EPS = 1e-6
ROPE_THETA = 10000.0
Q_BLOCK = 128
NA_HEADS = 8
NA_WIN_ROWS = 8
NA_WIN_COLS = 16
MLA_HEADS = 8
MLA_Q_RANK = 384
MLA_KV_RANK = 256
MLA_NOPE_DIM = 64
MLA_ROPE_DIM = 32
MLA_V_DIM = 64
GQA_HEADS = 16
GQA_KV_HEADS = 4

N_EVEN = (DEPTH + 1) // 2
N_ODD = DEPTH // 2
NA_WIDTH = NA_HEADS * HEAD_DIM
AB_IN_WIDTH = 3 * NA_WIDTH + MLA_Q_RANK + MLA_KV_RANK + MLA_ROPE_DIM
AB_OUT_WIDTH = NA_WIDTH + MLA_HEADS * MLA_V_DIM
C_OUT_WIDTH = GQA_HEADS * HEAD_DIM
C_IN_WIDTH = C_OUT_WIDTH + 2 * GQA_KV_HEADS * HEAD_DIM

kernel_name = "hybrid_grid_encoder_na_mla_gqa"


def _rmsnorm(x, g):
    xf = x.astype(jnp.float32)
    y = xf * lax.rsqrt(jnp.mean(xf * xf, axis=-1, keepdims=True) + NORM_EPS)
    return (y * g.astype(jnp.float32)).astype(x.dtype)


def _swiglu(h, wg, wu, wd):
    return (jax.nn.silu(h @ wg) * (h @ wu)) @ wd


def _axial_rope(n, rot_dim):
    t = jnp.arange(n)
    row = (t // GRID_W).astype(jnp.float32)
    col = (t % GRID_W).astype(jnp.float32)
    axis_dim = rot_dim // 2
    inv = ROPE_THETA ** (-jnp.arange(0, axis_dim, 2, dtype=jnp.float32) / axis_dim)
    ang = jnp.concatenate([row[:, None] * inv, col[:, None] * inv], axis=-1)
    return jnp.cos(ang), jnp.sin(ang)


def _apply_rope(x, cos, sin):
    shape = (x.shape[1],) + (1,) * (x.ndim - 3) + (x.shape[-1] // 2,)
    c = cos.reshape(shape).astype(x.dtype)
    s = sin.reshape(shape).astype(x.dtype)
    x1, x2 = jnp.split(x, 2, axis=-1)
    return jnp.concatenate([x1 * c - x2 * s, x2 * c + x1 * s], axis=-1)


def _neighbourhood_attention(q, k, v, rel_bias):
    b, n, h, dh = q.shape
    rows = n // GRID_W
    kh = min(NA_WIN_ROWS, rows)
    kw = NA_WIN_COLS
    qg = q.reshape(b, rows, GRID_W, h, dh)
    kg = k.reshape(b, rows, GRID_W, h, dh)
    vg = v.reshape(b, rows, GRID_W, h, dh)
    cols = jnp.arange(GRID_W)
    col_start = jnp.clip(cols - kw // 2, 0, GRID_W - kw)
    col_idx = col_start[:, None] + jnp.arange(kw)[None, :]
    dc = col_idx - cols[:, None] + (NA_WIN_COLS - 1)
    scale = dh ** -0.5

    def row_step(r):
        rs = jnp.clip(r - kh // 2, 0, rows - kh)
        q_r = lax.dynamic_index_in_dim(qg, r, axis=1, keepdims=False)
        k_r = lax.dynamic_slice_in_dim(kg, rs, kh, axis=1)[:, :, col_idx]
        v_r = lax.dynamic_slice_in_dim(vg, rs, kh, axis=1)[:, :, col_idx]
        dr = rs + jnp.arange(kh) - r + (NA_WIN_ROWS - 1)
        bias = rel_bias[:, dr[:, None, None], dc[None, :, :]]
        s = (jnp.einsum('bchd,bicjhd->bhcij', q_r, k_r).astype(jnp.float32) * scale
             + bias.transpose(0, 2, 1, 3).astype(jnp.float32)[None])
        p = jax.nn.softmax(s.reshape(b, h, GRID_W, kh * kw), axis=-1)
        p = p.reshape(b, h, GRID_W, kh, kw).astype(v.dtype)
        return jnp.einsum('bhcij,bicjhd->bchd', p, v_r)

    out = lax.map(row_step, jnp.arange(rows))
    return out.transpose(1, 0, 2, 3, 4).reshape(b, n, h * dh)


def _mla(cq, ckv, kr_raw, q_norm, w_uq, kv_norm, w_ukv, cos, sin):
    b, n, _ = cq.shape
    q = (_rmsnorm(cq, q_norm) @ w_uq).reshape(b, n, MLA_HEADS, MLA_NOPE_DIM + MLA_ROPE_DIM)
    q_nope = q[..., :MLA_NOPE_DIM]
    q_rope = _apply_rope(q[..., MLA_NOPE_DIM:], cos, sin)
    kv = (_rmsnorm(ckv, kv_norm) @ w_ukv).reshape(b, n, MLA_HEADS, MLA_NOPE_DIM + MLA_V_DIM)
    k_nope = kv[..., :MLA_NOPE_DIM]
    v = kv[..., MLA_NOPE_DIM:]
    k_rope = _apply_rope(kr_raw, cos, sin)
    nb = n // Q_BLOCK
    scale = (MLA_NOPE_DIM + MLA_ROPE_DIM) ** -0.5

    def to_blocks(t):
        return jnp.moveaxis(t.reshape((b, nb, Q_BLOCK) + t.shape[2:]), 1, 0)

    def block(qs):
        qn, qr = qs
        s = (jnp.einsum('bqhd,bnhd->bhqn', qn, k_nope)
             + jnp.einsum('bqhr,bnr->bhqn', qr, k_rope))
        p = jax.nn.softmax(s.astype(jnp.float32) * scale, axis=-1).astype(v.dtype)
        return jnp.einsum('bhqn,bnhd->bqhd', p, v)

    out = lax.map(block, (to_blocks(q_nope), to_blocks(q_rope)))
    return jnp.moveaxis(out, 0, 1).reshape(b, n, MLA_HEADS * MLA_V_DIM)


def _mixer_ab(h, w_in, na_bias, q_norm, w_uq, kv_norm, w_ukv, w_out, cos, sin):
    b, n, _ = h.shape
    proj = h @ w_in
    splits = [NA_WIDTH, 2 * NA_WIDTH, 3 * NA_WIDTH, 3 * NA_WIDTH + MLA_Q_RANK,
              3 * NA_WIDTH + MLA_Q_RANK + MLA_KV_RANK]
    qa, ka, va, cq, ckv, kr = jnp.split(proj, splits, axis=-1)

    def heads(t):
        return t.reshape(b, n, NA_HEADS, HEAD_DIM)

    oa = _neighbourhood_attention(heads(qa), heads(ka), heads(va), na_bias)
    ob = _mla(cq, ckv, kr, q_norm, w_uq, kv_norm, w_ukv, cos, sin)
    return jnp.concatenate([oa, ob], axis=-1) @ w_out


def _gqa_axial(h, w_in, q_norm, k_norm, w_out, cos, sin):
    b, n, _ = h.shape
    proj = h @ w_in
    q, k, v = jnp.split(proj, [C_OUT_WIDTH, C_OUT_WIDTH + GQA_KV_HEADS * HEAD_DIM], axis=-1)
    q = _apply_rope(_rmsnorm(q.reshape(b, n, GQA_HEADS, HEAD_DIM), q_norm), cos, sin)
    k = _apply_rope(_rmsnorm(k.reshape(b, n, GQA_KV_HEADS, HEAD_DIM), k_norm), cos, sin)
    v = v.reshape(b, n, GQA_KV_HEADS, HEAD_DIM)
    g = GQA_HEADS // GQA_KV_HEADS
    nb = n // Q_BLOCK
    scale = HEAD_DIM ** -0.5
    qb = jnp.moveaxis(q.reshape(b, nb, Q_BLOCK, GQA_KV_HEADS, g, HEAD_DIM), 1, 0)

    def block(qblk):
        s = jnp.einsum('bqkgd,bnkd->bkgqn', qblk, k)
        p = jax.nn.softmax(s.astype(jnp.float32) * scale, axis=-1).astype(v.dtype)
        return jnp.einsum('bkgqn,bnkd->bqkgd', p, v)

    out = lax.map(block, qb)
    o = jnp.moveaxis(out, 0, 1).reshape(b, n, C_OUT_WIDTH)
    return o @ w_out


def _trunk(x, norm_ffn1, ffn1_wg, ffn1_wu, ffn1_wd, norm_mix,
           ab_w_in, ab_na_bias, ab_q_norm, ab_w_uq, ab_kv_norm, ab_w_ukv, ab_w_out,
           c_w_in, c_q_norm, c_k_norm, c_w_out,
           norm_ffn2, ffn2_wg, ffn2_wu, ffn2_wd, final_norm):
    n = x.shape[1]
    cos_b, sin_b = _axial_rope(n, MLA_ROPE_DIM)
    cos_c, sin_c = _axial_rope(n, HEAD_DIM)
    for i in range(DEPTH):
        x = x + 0.5 * _swiglu(_rmsnorm(x, norm_ffn1[i]), ffn1_wg[i], ffn1_wu[i], ffn1_wd[i])
        h = _rmsnorm(x, norm_mix[i])
        j = i // 2
        if i % 2 == 0:
            x = x + _mixer_ab(h, ab_w_in[j], ab_na_bias[j], ab_q_norm[j], ab_w_uq[j],
                              ab_kv_norm[j], ab_w_ukv[j], ab_w_out[j], cos_b, sin_b)
        else:
            x = x + _gqa_axial(h, c_w_in[j], c_q_norm[j], c_k_norm[j], c_w_out[j], cos_c, sin_c)
        x = x + 0.5 * _swiglu(_rmsnorm(x, norm_ffn2[i]), ffn2_wg[i], ffn2_wu[i], ffn2_wd[i])
    return _rmsnorm(x, final_norm)


def setup_inputs(seed: int = 0) -> dict:
    key = jax.random.key(seed)
    ks = jax.random.split(key, 24)

    def dense(k, shape, fan_in):
        return jax.random.normal(k, shape, jnp.float32) * (fan_in ** -0.5)

    def gain(k, shape):
        return 1.0 + 0.02 * jax.random.normal(k, shape, jnp.float32)

    return {
        "x_prompt": jax.random.normal(ks[0], (BATCH, SEQ, D_MODEL), jnp.float32),
        "x_sample": jax.random.normal(ks[1], (DEC_BATCH, DEC_SEQ, D_MODEL), jnp.float32),
        "norm_ffn1": gain(ks[2], (DEPTH, D_MODEL)),
        "ffn1_wg": dense(ks[3], (DEPTH, D_MODEL, D_FF), D_MODEL),
        "ffn1_wu": dense(ks[4], (DEPTH, D_MODEL, D_FF), D_MODEL),
        "ffn1_wd": dense(ks[5], (DEPTH, D_FF, D_MODEL), D_FF),
        "norm_mix": gain(ks[6], (DEPTH, D_MODEL)),
        "ab_w_in": dense(ks[7], (N_EVEN, D_MODEL, AB_IN_WIDTH), D_MODEL),
        "ab_na_bias": 0.02 * jax.random.normal(ks[8], (N_EVEN, NA_HEADS, 2 * NA_WIN_ROWS - 1, 2 * NA_WIN_COLS - 1), jnp.float32),
        "ab_q_norm": gain(ks[9], (N_EVEN, MLA_Q_RANK)),
        "ab_w_uq": dense(ks[10], (N_EVEN, MLA_Q_RANK, MLA_HEADS * (MLA_NOPE_DIM + MLA_ROPE_DIM)), MLA_Q_RANK),
        "ab_kv_norm": gain(ks[11], (N_EVEN, MLA_KV_RANK)),
        "ab_w_ukv": dense(ks[12], (N_EVEN, MLA_KV_RANK, MLA_HEADS * (MLA_NOPE_DIM + MLA_V_DIM)), MLA_KV_RANK),
        "ab_w_out": dense(ks[13], (N_EVEN, AB_OUT_WIDTH, D_MODEL), AB_OUT_WIDTH),
        "c_w_in": dense(ks[14], (N_ODD, D_MODEL, C_IN_WIDTH), D_MODEL),
        "c_q_norm": gain(ks[15], (N_ODD, HEAD_DIM)),
        "c_k_norm": gain(ks[16], (N_ODD, HEAD_DIM)),
        "c_w_out": dense(ks[17], (N_ODD, C_OUT_WIDTH, D_MODEL), C_OUT_WIDTH),
        "norm_ffn2": gain(ks[18], (DEPTH, D_MODEL)),
        "ffn2_wg": dense(ks[19], (DEPTH, D_MODEL, D_FF), D_MODEL),
        "ffn2_wu": dense(ks[20], (DEPTH, D_MODEL, D_FF), D_MODEL),
        "ffn2_wd": dense(ks[21], (DEPTH, D_FF, D_MODEL), D_FF),
        "final_norm": gain(ks[22], (D_MODEL,)),
    }


def reference(x_prompt, x_sample, norm_ffn1, ffn1_wg, ffn1_wu, ffn1_wd, norm_mix,
              ab_w_in, ab_na_bias, ab_q_norm, ab_w_uq, ab_kv_norm, ab_w_ukv, ab_w_out,
              c_w_in, c_q_norm, c_k_norm, c_w_out,
              norm_ffn2, ffn2_wg, ffn2_wu, ffn2_wd, final_norm):
    y_prompt = _trunk(x_prompt, norm_ffn1, ffn1_wg, ffn1_wu, ffn1_wd, norm_mix,
                      ab_w_in, ab_na_bias, ab_q_norm, ab_w_uq, ab_kv_norm, ab_w_ukv, ab_w_out,
                      c_w_in, c_q_norm, c_k_norm, c_w_out,
                      norm_ffn2, ffn2_wg, ffn2_wu, ffn2_wd, final_norm)
    y_sample = _trunk(x_sample, norm_ffn1, ffn1_wg, ffn1_wu, ffn1_wd, norm_mix,
                      ab_w_in, ab_na_bias, ab_q_norm, ab_w_uq, ab_kv_norm, ab_w_ukv, ab_w_out,
                      c_w_in, c_q_norm, c_k_norm, c_w_out,
                      norm_ffn2, ffn2_wg, ffn2_wu, ffn2_wd, final_norm)
    return (y_prompt, y_sample)
```

```python
import numpy as np
from contextlib import ExitStack
import concourse.bass as bass
import concourse.mybir as mybir
from concourse.bass_utils import run_bass_kernel_spmd

F32 = mybir.dt.float32
BF16 = mybir.dt.bfloat16
AF = mybir.ActivationFunctionType
ALU = mybir.AluOpType

T = 4096
SEG = 2048
TT = 512
NFC = 22
EPS = 1e-6
NEG = -30000.0
N_CORES = 8


class Prog:
    ENG = ("pe", "act", "dve", "pool", "sp")

    def __init__(self, nc, gstack, same_engine_sync=True):
        self.nc = nc
        self.gstack = gstack
        self.streams = {e: [] for e in self.ENG}
        self.count = {e: 0 for e in self.ENG}
        self.waited = {e: {} for e in self.ENG}
        self.res = {}
        self.dma_count = {}
        self.same_engine_sync = same_engine_sync
        self.sems = {}

    def _need(self, eng, reads, writes):
        need = {}

        def add(ev):
            if ev is None:
                return
            k, v = ev
            if k == eng and (eng == "pe" or not self.same_engine_sync):
                return
            if k in self.dma_count:
                v = self.dma_count[k]
            if need.get(k, 0) < v:
                need[k] = v

        for r in reads:
            st = self.res.get(r)
            if st:
                add(st[0])
        for w in writes:
            st = self.res.get(w)
            if st:
                add(st[0])
                for ev in st[1]:
                    add(ev)
        waits = []
        for k, v in need.items():
            if self.waited[eng].get(k, 0) >= v:
                continue
            self.waited[eng][k] = v
            waits.append((k, v))
        return waits

    def _record(self, ev, reads, writes):
        for r in reads:
            st = self.res.setdefault(r, [None, []])
            st[1].append(ev)
        for w in writes:
            self.res[w] = [ev, []]

    def op(self, eng, fn, reads=(), writes=(), sig=True):
        waits = self._need(eng, reads, writes)
        if sig:
            self.count[eng] += 1
            ev = (eng, self.count[eng])
            self._record(ev, reads, writes)
            self.streams[eng].append((waits, fn, (eng, 1), True))
        else:
            self.streams[eng].append((waits, fn, None, True))

    def dma(self, q, key, out, in_, reads=(), writes=(), **kw):
        waits = self._need(q, reads, writes)
        self.dma_count[key] = self.dma_count.get(key, 0) + 16
        ev = (key, self.dma_count[key])
        self._record(ev, reads, writes)

        def fn(e, out=out, in_=in_, kw=kw):
            return e.dma_start(out=out, in_=in_, **kw)
        self.streams[q].append((waits, fn, (key, 16), False))

    def barrier(self):
        keys = list(self.ENG) + list(self.dma_count.keys())
        for eng in self.ENG:
            waits = []
            for k in keys:
                v = self.dma_count.get(k, self.count.get(k, 0))
                if k == eng:
                    continue
                if v and self.waited[eng].get(k, 0) < v:
                    self.waited[eng][k] = v
                    waits.append((k, v))
            self.streams[eng].append((waits, None, None, False))
        self.res = {}

    def emit(self, stack):
        nc = self.nc
        for k in sorted(set(self.ENG) | set(self.dma_count.keys()), key=str):
            if k not in self.sems:
                self.sems[k] = self.gstack.enter_context(nc.semaphore("s_" + str(k)))
        block = stack.enter_context(nc.Block())
        sems = self.sems
        streams = self.streams
        self.streams = {e: [] for e in self.ENG}

        def run(e, stream):
            for waits, fn, inc, fuse in stream:
                fused = waits[-1] if (fuse and fn is not None and waits) else None
                for k, v in (waits[:-1] if fused else waits):
                    e.wait_ge(sems[k], v)
                if fn is None:
                    continue
                ins = fn(e)
                if fused is not None:
                    ins._wait_ge(sems[fused[0]], fused[1])
                if inc is not None:
                    ins.then_inc(sems[inc[0]], inc[1])

        @block.tensor
        def _(e):
            run(e, streams["pe"])

        @block.scalar
        def _(e):
            run(e, streams["act"])

        @block.vector
        def _(e):
            run(e, streams["dve"])

        @block.gpsimd
        def _(e):
            run(e, streams["pool"])

        @block.sync
        def _(e):
            run(e, streams["sp"])


class Ring:
    def __init__(self, items):
        self.items = items
        self.i = 0

    def next(self):
        it = self.items[self.i % len(self.items)]
        self.i += 1
        return it

    def peek(self, k=0):
        return self.items[(self.i + k) % len(self.items)]


class WStream:
    def __init__(self, P, name, slots, srcs, look=None, queue="pool"):
        self.P, self.name, self.slots, self.srcs = P, name, slots, srcs
        self.look = len(slots) - 1 if look is None else look
        self.issued = 0
        self.queue = queue

    def get(self, i):
        P = self.P
        hi = min(len(self.srcs), i + self.look + 1)
        while self.issued < hi:
            j = self.issued
            s = j % len(self.slots)
            P.dma(self.queue, f"w_{self.name}{s}", self.slots[s][:], self.srcs[j],
                  writes=[(self.name, s)], max_dma_last_dim=4096)
            self.issued += 1
        s = i % len(self.slots)
        return self.slots[s], (self.name, s)


class Builder:
    def __init__(self, debug=False, stop_after=None):
        self.debug = debug
        self.stop_after = stop_after
        self.nc = bass.Bass("TRN2", target_bir_lowering=False)
        self.uid = 0

    def din(self, name, shape, dt=F32):
        return self.nc.dram_tensor(name, list(shape), dt, kind="ExternalInput").ap()

    def sb(self, st, shape, dt, name=None):
        self.uid += 1
        return st.enter_context(self.nc.sbuf_tensor(f"{name or 't'}_{self.uid}", list(shape), dt))

    def pe_group(self, fns, reads, writes, pre=None):
        P = self.P
        n = len(fns)
        for i, fn in enumerate(fns):
            if i == n - 1 and n > 1:
                P.op("pe", fn, reads=reads, writes=writes, sig=True)
            elif i == 0:
                r0 = list(reads) + (list(pre[0]) if pre else [])
                w0 = list(writes) + (list(pre[1]) if pre else [])
                if n == 1:
                    waits = P._need("pe", r0, w0)
                    P.count["pe"] += 1
                    ev = ("pe", P.count["pe"])
                    P._record(ev, reads, writes)
                    P.streams["pe"].append((waits, fn, ("pe", 1), True))
                else:
                    P.op("pe", fn, reads=r0, writes=w0, sig=False)
            else:
                P.op("pe", fn, reads=(), writes=(), sig=False)

    def mm_group(self, out, pairs, reads, writes, pre=None):
        n = len(pairs)
        fns = []
        for i, (l, r) in enumerate(pairs):
            def fn(e, l=l, r=r, i=i):
                return e.matmul(out, lhsT=l, rhs=r, start=(i == 0), stop=(i == n - 1))
            fns.append(fn)
        self.pe_group(fns, reads, writes, pre=pre)

    def rstd_from_ss(self, ss_ps, ss_res, dim, lnv, rstd, rstd_res, parts=slice(0, 128)):
        P = self.P
        P.op("act", lambda e: e.activation(out=lnv[parts, :], in_=ss_ps[parts, :], func=AF.Ln,
                                           scale=1.0 / dim, bias=self.epsc[parts, 0:1]),
             reads=[ss_res], writes=[("lnv", id(lnv))])
        P.op("act", lambda e: e.activation(out=rstd[parts, :], in_=lnv[parts, :], func=AF.Exp, scale=-0.5),
             reads=[("lnv", id(lnv))], writes=[rstd_res])

    def norm_tile(self, srcs, src_res, gcols, dsts, dst_res, dim, lhsT, sc):
        P = self.P
        C = len(srcs)
        ssb, ssr = sc["ss"]
        for c in range(C):
            sq, sqr = sc["sq"].next()
            P.op("act", lambda e, c=c, sq=sq: e.activation(out=sq[:, :], in_=srcs[c], func=AF.Square),
                 reads=[src_res[c]], writes=[sqr])
            P.op("pe", lambda e, c=c, sq=sq: e.matmul(ssb[:, :], lhsT=lhsT, rhs=sq[:, :], start=(c == 0), stop=(c == C - 1)),
                 reads=[sqr], writes=[ssr])
        lnv, rstd, rstd_res = sc["lnv"], sc["rstd"], sc["rstd_res"]
        self.rstd_from_ss(ssb, ssr, dim, lnv, rstd, rstd_res)
        for c in range(C):
            P.op("dve", lambda e, c=c: e.scalar_tensor_tensor(out=dsts[c], in0=srcs[c], scalar=gcols[c], in1=rstd[:, :],
                                                               op0=ALU.mult, op1=ALU.mult),
                 reads=[src_res[c], rstd_res], writes=[dst_res[c]])

    def ffn_seg(self, fidx, normcol, xT, hb, ob, sc, wgu_s, wd_s, wbase):
        P = self.P
        psg, psu, psd = sc["psg"], sc["psu"], sc["psd"]
        for tt in range(4):
            tsl = slice(tt * TT, (tt + 1) * TT)
            self.norm_tile([xT[:, c, tsl] for c in range(8)], [("xT", c, tt) for c in range(8)],
                           [self.ng[:, normcol * 8 + c: normcol * 8 + c + 1] for c in range(8)],
                           [hb[:, c, tsl] for c in range(8)], [("hb", c, tt) for c in range(8)],
                           1024.0, self.ones_bf[:, :], sc)
        groups = [list(range(g, min(g + 4, NFC))) for g in range(0, NFC, 4)]
        for gi, grp in enumerate(groups):
            wds = []
            for fl, f in enumerate(grp):
                wslot, wres = wgu_s.get(wbase + f)
                dslot, dres = wd_s.get(wbase + f)
                wds.append((dslot, dres))
                ai = (gi % 2) * 4 + fl
                for tt in range(4):
                    tsl = slice(tt * TT, (tt + 1) * TT)
                    bg, bgr = psg.next()
                    bu, bur = psu.next()
                    hres = [("hb", c, tt) for c in range(8)]
                    self.mm_group(bg[:, :], [(wslot[:, kc * 128:(kc + 1) * 128], hb[:, kc, tsl]) for kc in range(8)],
                                  reads=[wres] + hres, writes=[bgr], pre=([], [bur]))
                    self.mm_group(bu[:, :], [(wslot[:, (8 + kc) * 128:(9 + kc) * 128], hb[:, kc, tsl]) for kc in range(8)],
                                  reads=[wres] + hres, writes=[bur], pre=([], [psg.peek()[1]]))
                    sg, sgr = sc["sg"].next()
                    P.op("act", lambda e, bg=bg, sg=sg: e.activation(out=sg[:, :], in_=bg[:, :], func=AF.Silu),
                         reads=[bgr], writes=[sgr])
                    P.op("dve", lambda e, sg=sg, bu=bu, ai=ai, tsl=tsl: e.tensor_tensor(out=ob[:, ai, tsl], in0=bu[:, :], in1=sg[:, :], op=ALU.mult),
                         reads=[bur, sgr], writes=[("ob", ai, tt)])
            for tt in range(4):
                for oc in range(8):
                    tsl = slice(tt * TT, (tt + 1) * TT)
                    if oc % 2 == 0:
                        bpair = [psd.next(), psd.next()]
                    bd, bdr = bpair[oc % 2]
                    pairs, rds = [], []
                    for fl, f in enumerate(grp):
                        ai = (gi % 2) * 4 + fl
                        pairs.append((wds[fl][0][:, oc * 128:(oc + 1) * 128], ob[:, ai, tsl]))
                        rds += [wds[fl][1], ("ob", ai, tt)]
                    self.mm_group(bd[:, :], pairs, reads=rds, writes=[bdr],
                                  pre=([], [bpair[1][1]]) if oc % 2 == 0 else ([], [psd.peek()[1]]))
                    P.op("dve", lambda e, bd=bd, oc=oc, tsl=tsl: e.scalar_tensor_tensor(
                        out=xT[:, oc, tsl], in0=bd[:, :], scalar=0.5, in1=xT[:, oc, tsl], op0=ALU.mult, op1=ALU.add),
                        reads=[bdr, ("xT", oc, tt)], writes=[("xT", oc, tt)])

    def front(self, fi):
        P, nc = self.P, self.nc
        first, last = fi == 0, fi == 2
        with ExitStack() as st:
            xT = self.sb(st, [128, 8, SEG], F32, "xT")
            hb = self.sb(st, [128, 8, SEG], BF16, "hb")
            ob = self.sb(st, [128, 8, SEG], BF16, "ob")
            stg = [self.sb(st, [128, 1024], F32, "stg") for _ in range(2)]
            sc = self.common_scratch(st)
            sc["sg"] = Ring([(self.sb(st, [128, TT], F32, "sg"), ("sg", i)) for i in range(2)])
            ps = self.ps
            sc["psg"] = Ring([(ps[0], ("ps", 0)), (ps[1], ("ps", 1))])
            sc["psu"] = Ring([(ps[2], ("ps", 2)), (ps[3], ("ps", 3))])
            sc["psd"] = Ring([(ps[4], ("ps", 4)), (ps[5], ("ps", 5)), (ps[6], ("ps", 6)), (ps[7], ("ps", 7))])
            sc["ss"] = (ps[6], ("ps", 6))
            pst = Ring([(ps[4], ("ps", 4)), (ps[5], ("ps", 5)), (ps[7], ("ps", 7))])
            ffns = []
            if not first:
                ffns.append(2 * (fi - 1) + 1)
            if not last:
                ffns.append(2 * fi)
            srcs_gu, srcs_d = [], []
            for s in range(2):
                for f_ in ffns:
                    for f in range(NFC):
                        srcs_gu.append(self.WGU[f_, f])
                        srcs_d.append(self.WD[f_, f])
            wgu_s = WStream(P, "wgu", [self.sb(st, [128, 2048], BF16, "wgu") for _ in range(3)], srcs_gu)
            wd_s = WStream(P, "wd", [self.sb(st, [128, 1024], BF16, "wd") for _ in range(8)], srcs_d, look=4)
            if not first:
                wo_t = [self.sb(st, [128, 1024], BF16, "wo") for _ in range(8)]
                for oc in range(8):
                    P.dma("pool", f"w_wo{oc}", wo_t[oc][:, :], self.WO[fi - 1, oc], writes=[("wo", oc)], max_dma_last_dim=4096)
            wpos = 0
            for s in range(2):
                t0 = s * SEG
                if first:
                    for tk in range(16):
                        buf = stg[tk % 2]
                        P.dma("sp", f"stg{tk % 2}", buf[:, :], self.X[t0 + tk * 128: t0 + (tk + 1) * 128, :],
                              writes=[("stg", tk % 2)])
                        for half in range(2):
                            bank, br = pst.next()
                            fns = []
                            for q in range(4):
                                fc = half * 4 + q
                                fns.append(lambda e, bank=bank, q=q, fc=fc, buf=buf: e.transpose(
                                    out=bank[:, q * 128:(q + 1) * 128], in_=buf[:, fc * 128:(fc + 1) * 128], identity=self.ident[:, :]))
                            self.pe_group(fns, reads=[("stg", tk % 2)], writes=[br])
                            eng = "act" if half == 0 else "dve"
                            dst = xT[:, half * 4:(half + 1) * 4, tk * 128:(tk + 1) * 128]
                            src = bank[:, :].rearrange("p (a b) -> p a b", a=4)
                            wr = [("xT", half * 4 + q, tk // 4) for q in range(4)]
                            if eng == "act":
                                P.op("act", lambda e, dst=dst, src=src: e.activation(out=dst, in_=src, func=AF.Copy), reads=[br], writes=wr)
                            else:
                                P.op("dve", lambda e, dst=dst, src=src: e.tensor_copy(out=dst, in_=src), reads=[br], writes=wr)
                else:
                    for tt in range(4):
                        tsl = slice(tt * TT, (tt + 1) * TT)
                        g = s * 4 + tt
                        P.dma("sp", f"ld_ob{tt}", ob[:, :, tsl], self.os_tm[g], writes=[("ob", c, tt) for c in range(8)])
                        P.dma("sp", f"ld_xT{tt}", xT[:, :, tsl], self.xs_tm[g], writes=[("xT", c, tt) for c in range(8)])
                    for tt in range(4):
                        for oc in range(8):
                            wslot, wres = wo_t[oc], ("wo", oc)
                            tsl = slice(tt * TT, (tt + 1) * TT)
                            bd, bdr = sc["psd"].next()
                            self.mm_group(bd[:, :], [(wslot[:, kc * 128:(kc + 1) * 128], ob[:, kc, tsl]) for kc in range(8)],
                                          reads=[wres] + [("ob", kc, tt) for kc in range(8)], writes=[bdr],
                                          pre=([], [sc["psd"].peek()[1]]))
                            P.op("dve", lambda e, bd=bd, oc=oc, tsl=tsl: e.tensor_tensor(
                                out=xT[:, oc, tsl], in0=bd[:, :], in1=xT[:, oc, tsl], op=ALU.add),
                                reads=[bdr, ("xT", oc, tt)], writes=[("xT", oc, tt)])
                    self.ffn_seg(ffns[0], 3 * (fi - 1) + 2, xT, hb, ob, sc, wgu_s, wd_s, wpos)
                    wpos += NFC
                if not last:
                    self.ffn_seg(ffns[-1], 3 * fi, xT, hb, ob, sc, wgu_s, wd_s, wpos)
                    wpos += NFC
                    for tt in range(4):
                        tsl = slice(tt * TT, (tt + 1) * TT)
                        self.norm_tile([xT[:, c, tsl] for c in range(8)], [("xT", c, tt) for c in range(8)],
                                       [self.ng[:, (3 * fi + 1) * 8 + c:(3 * fi + 1) * 8 + c + 1] for c in range(8)],
                                       [hb[:, c, tsl] for c in range(8)], [("hb", c, tt) for c in range(8)],
                                       1024.0, self.ones_bf[:, :], sc)
                        g = s * 4 + tt
                        P.dma("sp", f"st_hs{tt}", self.hs_tm[g], hb[:, :, tsl], reads=[("hb", c, tt) for c in range(8)])
                        P.dma("sp", f"st_xs{tt}", self.xs_tm[g], xT[:, :, tsl], reads=[("xT", c, tt) for c in range(8)])
                else:
                    for tt in range(4):
                        tsl = slice(tt * TT, (tt + 1) * TT)
                        self.norm_tile([xT[:, c, tsl] for c in range(8)], [("xT", c, tt) for c in range(8)],
                                       [self.ng[:, 48 + c:48 + c + 1] for c in range(8)],
                                       [xT[:, c, tsl] for c in range(8)], [("xT", c, tt) for c in range(8)],
                                       1024.0, self.ones_bf[:, :], sc)
                    for tk in range(16):
                        buf = stg[tk % 2]
                        for half in range(2):
                            bank, br = pst.next()
                            fns = []
                            for q in range(4):
                                fc = half * 4 + q
                                fns.append(lambda e, bank=bank, q=q, fc=fc, tk=tk: e.transpose(
                                    out=bank[:, q * 128:(q + 1) * 128], in_=xT[:, fc, tk * 128:(tk + 1) * 128], identity=self.ident[:, :]))
                            self.pe_group(fns, reads=[("xT", half * 4 + qq, tk // 4) for qq in range(4)], writes=[br])
                            dst = buf[:, half * 512:(half + 1) * 512]
                            if half == 0:
                                P.op("act", lambda e, dst=dst, bank=bank: e.activation(out=dst, in_=bank[:, :], func=AF.Copy),
                                     reads=[br], writes=[("stg", tk % 2, 0)])
                            else:
                                P.op("dve", lambda e, dst=dst, bank=bank: e.tensor_copy(out=dst, in_=bank[:, :]),
                                     reads=[br], writes=[("stg", tk % 2, 1)])
                        P.dma("sp", f"st_y{tk % 2}", self.Y[t0 + tk * 128: t0 + (tk + 1) * 128, :], buf[:, :],
                              reads=[("stg", tk % 2, 0), ("stg", tk % 2, 1)])
            P.barrier()
            P.emit(st)

    def _touch_read(self, eng, reads):
        P = self.P
        ev = (eng, P.count[eng])
        for r in reads:
            stt = P.res.setdefault(r, [None, []])
            stt[1].append(ev)

    def common_scratch(self, st):
        sc = {}
        sc["sq"] = Ring([(self.sb(st, [128, TT], BF16, "sq"), ("sq", i)) for i in range(2)])
        sc["lnv"] = self.sb(st, [128, TT], F32, "lnv")
        sc["rstd"] = self.sb(st, [128, TT], F32, "rstd")
        sc["rstd_res"] = ("rstd", 0)
        return sc

    def attend(self, kts, kfn, qrhs, qres, vfn, obank, obr, sbanks, pbufs, expfn):
        P = self.P
        n = len(kts)
        for i in range(n + 2):
            if i < n:
                kt = kts[i]
                sbk, sres = sbanks[i % len(sbanks)]
                kl, kres = kfn(kt)
                P.op("pe", lambda e, sbk=sbk, kl=kl: e.matmul(sbk[:, :], lhsT=kl, rhs=qrhs, start=True, stop=True),
                     reads=[kres] + list(qres), writes=[sres])
            j = i - 1
            if 0 <= j < n:
                sbk, sres = sbanks[j % len(sbanks)]
                pb, pres = pbufs[j % len(pbufs)]
                expfn(j, kts[j], sbk, sres, pb, pres)
            j = i - 2
            if 0 <= j < n:
                pb, pres = pbufs[j % len(pbufs)]
                vl, vres = vfn(kts[j])
                P.op("pe", lambda e, vl=vl, pb=pb, j=j: e.matmul(obank[:, :], lhsT=vl, rhs=pb[:, :], start=(j == 0), stop=(j == n - 1)),
                     reads=[vres, pres], writes=[obr])

    def attend2(self, kts, kfn, qrhs, qres, vfn, obank, obr, sgroups, pbufs, G, expfn, extra_fn=None):
        P = self.P
        groups = [kts[i:i + G] for i in range(0, len(kts), G)]
        n = len(groups)
        for i in range(n + 2):
            if i < n:
                sg, sres = sgroups[i % len(sgroups)]
                for j, kt in enumerate(groups[i]):
                    kl, kres = kfn(kt)
                    if extra_fn is None:
                        P.op("pe", lambda e, sg=sg, kl=kl, j=j: e.matmul(sg[:, j * 512:(j + 1) * 512], lhsT=kl, rhs=qrhs, start=True, stop=True),
                             reads=[kres] + list(qres), writes=[sres])
                    else:
                        xl, xr, xres = extra_fn(kt)
                        P.op("pe", lambda e, sg=sg, kl=kl, j=j: e.matmul(sg[:, j * 512:(j + 1) * 512], lhsT=kl, rhs=qrhs, start=True, stop=False),
                             reads=[kres] + list(qres) + list(xres), writes=[sres], sig=False)
                        P.op("pe", lambda e, sg=sg, xl=xl, xr=xr, j=j: e.matmul(sg[:, j * 512:(j + 1) * 512], lhsT=xl, rhs=xr, start=False, stop=True),
                             reads=[kres] + list(qres) + list(xres), writes=[sres])
            j = i - 1
            if 0 <= j < n:
                sg, sres = sgroups[j % len(sgroups)]
                pb, pres = pbufs[j % len(pbufs)]
                w = len(groups[j]) * 512
                expfn(j, groups[j], sg[:, 0:w], sres, pb[:, 0:w], pres)
            j = i - 2
            if 0 <= j < n:
                pb, pres = pbufs[j % len(pbufs)]
                for jj, kt in enumerate(groups[j]):
                    vl, vres = vfn(kt)
                    first = (j == 0 and jj == 0)
                    last = (j == n - 1 and jj == len(groups[j]) - 1)
                    P.op("pe", lambda e, vl=vl, pb=pb, jj=jj, first=first, last=last: e.matmul(
                        obank[:, :], lhsT=vl, rhs=pb[:, jj * 512:(jj + 1) * 512], start=first, stop=last),
                        reads=[vres, pres], writes=[obr])

    def pipe_new(self, sgroups, pbufs, G):
        return {"sg": sgroups, "pb": pbufs, "G": G, "cnt": 0, "pend": []}

    def _pipe_S(self, it):
        P = self.P
        sg, sres = it["sg"]
        for j, kt in enumerate(it["grp"]):
            kl, kres = it["kfn"](kt)
            qrhs, qres = it["qrhs"], it["qres"]
            if it["extra_fn"] is None:
                P.op("pe", lambda e, sg=sg, kl=kl, j=j, qrhs=qrhs: e.matmul(sg[:, j * 512:(j + 1) * 512], lhsT=kl, rhs=qrhs, start=True, stop=True),
                     reads=[kres] + list(qres), writes=[sres])
            else:
                xl, xr, xres = it["extra_fn"](kt)
                P.op("pe", lambda e, sg=sg, kl=kl, j=j, qrhs=qrhs: e.matmul(sg[:, j * 512:(j + 1) * 512], lhsT=kl, rhs=qrhs, start=True, stop=False),
                     reads=[kres] + list(qres) + list(xres), writes=[sres], sig=False)
                P.op("pe", lambda e, sg=sg, xl=xl, xr=xr, j=j: e.matmul(sg[:, j * 512:(j + 1) * 512], lhsT=xl, rhs=xr, start=False, stop=True),
                     reads=[kres] + list(qres) + list(xres), writes=[sres])

    def _pipe_E(self, it):
        if it["exped"]:
            return
        it["exped"] = True
        sg, sres = it["sg"]
        pb, pres = it["pb"]
        w = len(it["grp"]) * 512
        it["expfn"](it["grp"], sg[:, 0:w], sres, pb[:, 0:w], pres)

    def _pipe_V(self, it):
        P = self.P
        pb, pres = it["pb"]
        obank, obr = it["obank"], it["obr"]
        n = len(it["grp"])
        for jj, kt in enumerate(it["grp"]):
            vl, vres = it["vfn"](kt)
            first = it["first"] and jj == 0
            last = it["last"] and jj == n - 1
            P.op("pe", lambda e, vl=vl, pb=pb, jj=jj, first=first, last=last, obank=obank: e.matmul(
                obank[:, :], lhsT=vl, rhs=pb[:, jj * 512:(jj + 1) * 512], start=first, stop=last),
                reads=[vres, pres], writes=[obr])
        if it["last"] and it["fin"] is not None:
            it["fin"]()

    def pipe_block(self, pipe, kts, kfn, qrhs, qres, vfn, obank, obr, expfn, fin, extra_fn=None):
        G = pipe["G"]
        groups = [kts[i:i + G] for i in range(0, len(kts), G)]
        for gi, grp in enumerate(groups):
            idx = pipe["cnt"]
            pipe["cnt"] += 1
            it = dict(grp=grp, kfn=kfn, qrhs=qrhs, qres=qres, vfn=vfn, obank=obank, obr=obr, expfn=expfn, extra_fn=extra_fn,
                      first=(gi == 0), last=(gi == len(groups) - 1), fin=fin, exped=False,
                      sg=pipe["sg"][idx % len(pipe["sg"])], pb=pipe["pb"][idx % len(pipe["pb"])])
            self._pipe_S(it)
            pend = pipe["pend"]
            pend.append(it)
            if len(pend) >= 2:
                self._pipe_E(pend[-2])
            if len(pend) >= 3:
                self._pipe_V(pend.pop(0))

    def pipe_flush(self, pipe):
        for it in pipe["pend"]:
            self._pipe_E(it)
        for it in pipe["pend"]:
            self._pipe_V(it)
        pipe["pend"] = []

    def finalize(self, obank, obr, rec, recr, dst, dres, base):
        P = self.P
        pr = slice(base, base + 64)
        P.op("dve", lambda e: e.reciprocal(out=rec[pr, :], in_=obank[64:128, :]), reads=[obr], writes=[recr])
        P.op("dve", lambda e: e.tensor_tensor(out=dst, in0=obank[0:64, :], in1=rec[pr, :], op=ALU.mult),
             reads=[obr, recr], writes=[dres])

    def load_H2(self, st):
        P = self.P
        H2 = self.sb(st, [128, 8, T], BF16, "H2")
        for tt in range(8):
            tsl = slice(tt * TT, (tt + 1) * TT)
            P.dma("sp", f"ld_H2_{tt}", H2[:, :, tsl], self.hs_tm[tt], writes=[("H2t", tt)])
        return H2

    def vproj(self, VA, nk, ncols, lhs_fn, lhs_res, wv, wvres, col_fn, banks):
        P = self.P
        per = TT // ncols
        cnt = 0
        for g in range(0, 32, per):
            bank, br = banks.next()
            for i in range(per):
                tk = g + i
                self.mm_group(bank[:, i * ncols:(i + 1) * ncols],
                              [(lhs_fn(kc, tk), wv[:, kc * ncols:(kc + 1) * ncols]) for kc in range(nk)],
                              reads=[wvres] + lhs_res, writes=[br])
            src3 = bank[:, :].rearrange("p (a b) -> p a b", a=per)
            cnt += 1
            for (va, vres, c0) in col_fn():
                eng = "act" if cnt % 2 == 0 else "dve"
                dst = va[:, g:g + per, 0:64]
                src = src3[:, :, c0:c0 + 64]
                if eng == "act":
                    P.op("act", lambda e, dst=dst, src=src: e.activation(out=dst, in_=src, func=AF.Copy), reads=[br], writes=[vres])
                else:
                    P.op("dve", lambda e, dst=dst, src=src: e.tensor_copy(out=dst, in_=src), reads=[br], writes=[vres])

    def gqa_phase(self):
        P, ps = self.P, self.ps
        with ExitStack() as st:
            ctab = [self.sb(st, [128, 2, TT], F32, "ctab") for _ in range(2)]
            ccnt = [0]

            def load_tab(tt):
                i = ccnt[0] % 2
                ccnt[0] += 1
                P.dma("sp", f"ld_ct{i}", ctab[i][:, :, :], self.ROPE_C.rearrange("a p t -> p a t")[:, :, tt * TT:(tt + 1) * TT],
                      writes=[("ctab", i)])
                return ctab[i], ("ctab", i)

            KT2 = [self.sb(st, [128, T], BF16, "KT2") for _ in range(2)]
            wk = [self.sb(st, [128, 2048], BF16, "wk") for _ in range(2)]
            for i in range(2):
                P.dma("pool", f"w_wk{i}", wk[i][:, :], self.WCK[i], writes=[("wk", i)], max_dma_last_dim=4096)
            wv = self.sb(st, [128, 2048], BF16, "wv")
            P.dma("pool", "w_wv", wv[:, :], self.WCV, writes=["wv"], max_dma_last_dim=4096)
            VA = [self.sb(st, [128, 32, 128], BF16, "VA") for _ in range(4)]
            VAx = [self.sb(st, [128, 32, 128], BF16, "VAx") for _ in range(4)]
            for i in range(4):
                P.op("pool", lambda e, i=i: e.memset(VA[i][:, :, 64:128], 1.0), writes=[("VA", i)])
            sc = self.common_scratch(st)
            QT = [self.sb(st, [128, T], BF16, "QT") for _ in range(2)]
            OSTh = [self.sb(st, [128, T], BF16, "OSTh") for i in range(2)]
            pbufs = [(self.sb(st, [128, 2 * TT], BF16, "pb"), ("pb", i)) for i in range(3)]
            rec = [self.sb(st, [128, TT], F32, "rec") for _ in range(2)]
            tlo = [self.sb(st, [128, TT], F32, "tlo") for _ in range(2)]
            thi = [self.sb(st, [128, TT], F32, "thi") for _ in range(2)]
            ssum = [self.sb(st, [128, TT], F32, "ssum") for _ in range(2)]
            h2r = [self.sb(st, [128, 8, TT], BF16, "h2t") for _ in range(2)]
            h2cnt = [0]

            def load_h2(tt):
                i = h2cnt[0] % 2
                h2cnt[0] += 1
                P.dma("sp", f"ld_h2t{i}", h2r[i][:, :, :], self.hs_tm[tt], writes=[("h2t", i)])
                return h2r[i], ("h2t", i)

            wq_s = WStream(P, "wq", [self.sb(st, [128, 2048], BF16, "wq") for _ in range(2)], [self.WCQ[i] for i in range(8)])
            sgroups = [(self.psall[:, 0:1024], ("psg", 0)), (self.psall[:, 1024:2048], ("psg", 1))]
            obanks = [(ps[4], ("ps", 4)), (ps[5], ("ps", 5))]
            praw, pswp = (ps[6], ("ps", 6)), (ps[7], ("ps", 7))

            psets = [((ps[4], ("ps", 4)), (ps[5], ("ps", 5))), ((ps[6], ("ps", 6)), (ps[7], ("ps", 7)))]
            scr = []
            for i in range(2):
                scr.append(dict(t1=self.sb(st, [128, TT], F32, "t1"), t2=self.sb(st, [128, TT], F32, "t2"), t3=self.sb(st, [128, TT], F32, "t3"),
                                lnv=self.sb(st, [128, TT], F32, "lnv"), rstd=self.sb(st, [128, TT], F32, "rstd")))
            pcnt = [0]

            def proj_rope(wslot, wres, h2t, h2res, tt, gc, gsc, dsts, ct, ctres):
                k = pcnt[0] % 2
                pcnt[0] += 1
                praw, pswp = psets[k]
                S = scr[k]
                t1, t2, t3, lnv, rstd = S["t1"], S["t2"], S["t3"], S["lnv"], S["rstd"]
                n1, n2, n3, nr = ("t1", k), ("t2", k), ("t3", k), ("rstdq", k)
                self.mm_group(praw[0][:, :], [(wslot[:, kc * 128:(kc + 1) * 128], h2t[:, kc, :]) for kc in range(8)],
                              reads=[wres, h2res], writes=[praw[1]], pre=([], [pswp[1]]))
                self.mm_group(pswp[0][:, :], [(wslot[:, (8 + kc) * 128:(9 + kc) * 128], h2t[:, kc, :]) for kc in range(8)],
                              reads=[wres, h2res], writes=[pswp[1]])
                sq, sqr = sc["sq"].next()
                P.op("act", lambda e: e.activation(out=sq[:, :], in_=praw[0][:, :], func=AF.Square), reads=[praw[1]], writes=[sqr])
                P.op("dve", lambda e: e.scalar_tensor_tensor(out=t1[:, :], in0=praw[0][:, :], scalar=self.hg[:, gc:gc + 1], in1=ct[:, 0, :],
                                                              op0=ALU.mult, op1=ALU.mult), reads=[praw[1], ctres, sqr], writes=[n1])
                P.op("dve", lambda e: e.scalar_tensor_tensor(out=t2[:, :], in0=pswp[0][:, :], scalar=self.hg[:, gsc:gsc + 1], in1=ct[:, 1, :],
                                                              op0=ALU.mult, op1=ALU.mult), reads=[pswp[1], ctres], writes=[n2])
                P.op("pe", lambda e: e.matmul(praw[0][:, :], lhsT=self.blk_bf[:, :], rhs=sq[:, :], start=True, stop=True),
                     reads=[sqr, n1], writes=[praw[1]])
                self.rstd_from_ss(praw[0], praw[1], 64.0, lnv, rstd, nr)
                P.op("pool", lambda e: e.tensor_tensor(out=t3[:, :], in0=t1[:, :], in1=t2[:, :], op=ALU.add), reads=[n1, n2], writes=[n3])
                for (dst, pr, dres) in dsts:
                    P.op("pool", lambda e, dst=dst, pr=pr: e.tensor_tensor(out=dst, in0=t3[pr, :], in1=rstd[pr, :], op=ALU.mult),
                         reads=[n3, nr], writes=[dres])

            vb = Ring([(ps[4], ("ps", 4)), (ps[5], ("ps", 5))])
            vcnt = 0
            for tt in range(8):
                tsl = slice(tt * TT, (tt + 1) * TT)
                h2t, h2res = load_h2(tt)
                ct, ctres = load_tab(tt)
                for kch in range(2):
                    proj_rope(wk[kch], ("wk", kch), h2t, h2res, tt, 2, 3,
                              [(KT2[kch][:, tsl], slice(0, 128), ("KTd", kch, tt))], ct, ctres)
                for half in range(2):
                    bank, br = vb.next()
                    for i in range(2):
                        tk = half * 2 + i
                        self.mm_group(bank[:, i * 256:(i + 1) * 256],
                                      [(h2t[:, kc, tk * 128:(tk + 1) * 128], wv[:, kc * 256:(kc + 1) * 256]) for kc in range(8)],
                                      reads=["wv", h2res], writes=[br])
                    src3 = bank[:, :].rearrange("p (a b) -> p a b", a=2)
                    vcnt += 1
                    for kv in range(4):
                        g = tt * 4 + half * 2
                        dst = VA[kv][:, g:g + 2, 0:64]
                        src = src3[:, :, kv * 64:(kv + 1) * 64]
                        if vcnt % 2 == 0:
                            P.op("act", lambda e, dst=dst, src=src: e.activation(out=dst, in_=src, func=AF.Copy), reads=[br], writes=[("VA", kv)])
                        else:
                            P.op("dve", lambda e, dst=dst, src=src: e.tensor_copy(out=dst, in_=src), reads=[br], writes=[("VA", kv)])
            for kv in range(4):
                P.op("dve", lambda e, kv=kv: e.tensor_scalar(out=VAx[kv][:, :, :], in0=VA[kv][:, :, :], scalar1=self.maskT[:, 4:5], scalar2=None, op0=ALU.mult),
                     reads=[("VA", kv)], writes=[("VAx", kv)])
            icnt = [0]

            def plan_q(qc, tt):
                wslot, wres = wq_s.get(qc)
                h2t, h2res = load_h2(tt)
                ct, ctres = load_tab(tt)
                proj_rope(wslot, wres, h2t, h2res, tt, 0, 1,
                          [(QT[qc % 2][:, tt * TT:(tt + 1) * TT], slice(0, 128), ("QT", qc % 2, tt))], ct, ctres)

            scale = 64.0 ** -0.5
            lvl = 3
            for tt in range(8 if lvl >= 2 else 0):
                plan_q(0, tt)
            for qc in range(8 if lvl >= 3 else 0):
                kch = qc // 4
                noint = 1
                if noint and qc + 1 < 8:
                    for tt in range(8):
                        plan_q(qc + 1, tt)
                obk4 = [(ps[4 + i], ("ps", 4 + i)) for i in range(4)]
                items = [(tt, kt) for tt in range(8) for kt in range(32)]
                pend = []

                def stage_S(it):
                    tt, kt, idx = it
                    sg, sres = sgroups[idx % 2]
                    qt = QT[qc % 2]
                    for hh in range(2):
                        pr = slice(hh * 64, hh * 64 + 64)
                        P.op("pe", lambda e, sg=sg, hh=hh, pr=pr, kt=kt, tt=tt, qt=qt, kch=kch: e.matmul(
                            sg[:, hh * 512:(hh + 1) * 512], lhsT=KT2[kch][pr, kt * 128:(kt + 1) * 128], rhs=qt[pr, tt * TT:(tt + 1) * TT],
                            start=True, stop=True),
                            reads=[("KTd", kch, kt // 4), ("QT", qc % 2, tt)], writes=[sres])

                def stage_E(it):
                    tt, kt, idx = it
                    sg, sres = sgroups[idx % 2]
                    pb, pres = pbufs[idx % 3]
                    P.op("act", lambda e, sg=sg, pb=pb: e.activation(out=pb[:, :], in_=sg, func=AF.Exp, scale=scale), reads=[sres], writes=[pres])

                def stage_V(it):
                    tt, kt, idx = it
                    pb, pres = pbufs[idx % 3]
                    qseg = tt // 4
                    for hh in range(2):
                        kv = 2 * kch + hh
                        if kt // 16 != qseg:
                            va, vres = VAx[kv], ("VAx", kv)
                        else:
                            va, vres = VA[kv], ("VA", kv)
                        for half in range(2):
                            ob_, obr = obk4[hh * 2 + half]
                            pr = slice(half * 64, half * 64 + 64)
                            P.op("pe", lambda e, ob_=ob_, va=va, pr=pr, kt=kt, pb=pb, hh=hh: e.matmul(
                                ob_[:, :], lhsT=va[pr, kt, :], rhs=pb[pr, hh * 512:(hh + 1) * 512], start=(kt == 0), stop=(kt == 31)),
                                reads=[vres, pres], writes=[obr])
                    if kt == 31:
                        for hh in range(2):
                            lo, lor = obk4[hh * 2]
                            P.op("dve", lambda e, hh=hh, lo=lo: e.tensor_copy(out=tlo[hh][:, :], in_=lo[:, :]), reads=[lor], writes=[("tlo", hh)])
                        for hh in range(2):
                            hi, hir = obk4[hh * 2 + 1]
                            P.op("act", lambda e, hh=hh, hi=hi: e.activation(out=thi[hh][:, :], in_=hi[:, :], func=AF.Copy), reads=[hir], writes=[("thi", hh)])
                        for hh in range(2):
                            P.op("pool", lambda e, hh=hh: e.tensor_tensor(out=ssum[hh][:, :], in0=tlo[hh][:, :], in1=thi[hh][:, :], op=ALU.add),
                                 reads=[("tlo", hh), ("thi", hh)], writes=[("ssum", hh)])
                            P.op("dve", lambda e, hh=hh: e.reciprocal(out=rec[hh][0:64, :], in_=ssum[hh][64:128, :]), reads=[("ssum", hh)], writes=[("rec", hh)])
                            P.op("pool", lambda e, hh=hh, tt=tt: e.tensor_tensor(out=OSTh[hh][0:64, tt * TT:(tt + 1) * TT], in0=ssum[hh][0:64, :], in1=rec[hh][0:64, :], op=ALU.mult),
                                 reads=[("ssum", hh), ("rec", hh)], writes=[("OSTh", hh)])

                for (tt, kt) in items:
                    it = (tt, kt, icnt[0])
                    icnt[0] += 1
                    stage_S(it)
                    pend.append(it)
                    if len(pend) >= 2:
                        stage_E(pend[-2])
                    if len(pend) >= 3:
                        stage_V(pend.pop(0))
                stage_E(pend[-1])
                for it in pend:
                    stage_V(it)
                for hh in range(2):
                    P.dma("pool", f"st_os{hh}", self.os_w[hh * 64:(hh + 1) * 64, qc], OSTh[hh][0:64, :].rearrange("p (g t) -> p g t", g=8),
                          reads=[("OSTh", hh)])
            P.barrier()
            P.emit(st)

    def na_phase(self, H2, st):
        P, ps = self.P, self.ps
        wav = self.sb(st, [128, 4096], BF16, "wav")
        for i in range(2):
            P.dma("pool", "w_wav", wav[:, i * 2048:(i + 1) * 2048], self.WAVA[:, i * 2048:(i + 1) * 2048], writes=["wav"], max_dma_last_dim=4096)
        wqk_s = WStream(P, "wqk", [self.sb(st, [128, 2048], BF16, "wqk") for _ in range(2)], [self.WAQK2[j] for j in range(4)])
        wqk_s.get(0)
        VAall = self.sb(st, [128, 8, 32, 128], BF16, "VAall")
        P.op("pool", lambda e: e.memset(VAall[:, :, :, 64:128].rearrange("p h k d -> p (h k) d"), 1.0), writes=["VAones"])
        QA = [self.sb(st, [128, T], BF16, "QA") for _ in range(2)]
        KA = [self.sb(st, [128, T], BF16, "KA") for _ in range(2)]
        for i in range(2):
            P.dma("pool", "ld_oh", KA[i][64:128, :], self.NA_OH, writes=[("KAc", i)], max_dma_last_dim=4096)
            P.dma("pool", "ld_rm", QA[i][64:128, :], self.NA_RM, writes=[("QAc", i)], max_dma_last_dim=4096)
        TH = self.sb(st, [128, 22 * 64], F32, "TH")
        TH8 = [self.sb(st, [128, 22 * 64], BF16, "TH8") for _ in range(2)]
        idb = self.sb(st, [128, 128], BF16, "idb")
        P.op("dve", lambda e: e.tensor_copy(out=idb[:, :], in_=self.ident[:, :]), writes=["idb"])
        OST = [self.sb(st, [128, T], BF16, "OSTn") for _ in range(1)]
        pbufs = [(self.sb(st, [128, 2 * TT], BF16, "pbn"), ("pbn", i)) for i in range(3)]
        rec = self.sb(st, [128, TT], F32, "recn")
        sgroups = [(self.psall[:, 0:1024], ("psg", 0)), (self.psall[:, 1024:2048], ("psg", 1))]
        obks = [(ps[4], ("ps", 4)), (ps[5], ("ps", 5))]
        pring = Ring([(ps[4 + i], ("ps", 4 + i)) for i in range(4)])
        hall = [("H2t", t_) for t_ in range(8)]
        pipe = self.pipe_new(sgroups, pbufs, 2)
        for tk in range(32):
            bank, br = pring.next()
            self.mm_group(bank[:, :], [(H2[:, kc, tk * 128:(tk + 1) * 128], wav[:, kc * 512:(kc + 1) * 512]) for kc in range(8)],
                          reads=["wav", ("H2t", tk // 4)], writes=[br], pre=([], [pring.peek()[1]]))
            dst = VAall[:, :, tk, 0:64]
            src = bank[:, :].rearrange("p (h d) -> p h d", h=8)
            if tk % 2 == 0:
                P.op("act", lambda e, dst=dst, src=src: e.activation(out=dst, in_=src, func=AF.Copy), reads=[br], writes=[("VAn", tk)])
            else:
                P.op("dve", lambda e, dst=dst, src=src: e.tensor_copy(out=dst, in_=src), reads=[br], writes=[("VAn", tk)])
        vres_all = ["VAones"] + [("VAn", tk) for tk in range(32)]
        for j in range(4):
            wslot, wres = wqk_s.get(j)
            for tt in range(8):
                tsl = slice(tt * TT, (tt + 1) * TT)
                hres = [("H2t", tt)]
                bq, bqr = pring.next()
                bk, bkr = pring.next()
                self.mm_group(bq[:, :], [(wslot[:, kc * 128:(kc + 1) * 128], H2[:, kc, tsl]) for kc in range(8)],
                              reads=[wres] + hres, writes=[bqr], pre=([], [bkr]))
                self.mm_group(bk[:, :], [(wslot[:, (8 + kc) * 128:(9 + kc) * 128], H2[:, kc, tsl]) for kc in range(8)],
                              reads=[wres] + hres, writes=[bkr], pre=([], [pring.peek()[1]]))
                for hb_ in range(2):
                    pr = slice(hb_ * 64, hb_ * 64 + 64)
                    P.op("act", lambda e, bq=bq, tsl=tsl, hb_=hb_, pr=pr: e.activation(out=QA[hb_][0:64, tsl], in_=bq[pr, :], func=AF.Copy),
                         reads=[bqr], writes=[("QA", hb_, tt)])
                    P.op("dve", lambda e, bk=bk, tsl=tsl, hb_=hb_, pr=pr: e.tensor_copy(out=KA[hb_][0:64, tsl], in_=bk[pr, :]),
                         reads=[bkr], writes=[("KA", hb_, tt)])
            for hb_ in range(2):
                h = 2 * j + hb_
                P.dma("sp", "ld_th", TH[:, :], self.NA_T[h], writes=["TH"])
                P.op("dve", lambda e, hb_=hb_: e.tensor_scalar(out=TH8[hb_][:, :], in0=TH[:, :], scalar1=8.0, scalar2=None, op0=ALU.mult),
                     reads=["TH"], writes=[("TH8", hb_)])
            for hb_ in range(2):
                h = 2 * j + hb_
                for b in range(8):
                    kts = [4 * b - 2 + jj for jj in range(8) if 0 <= 4 * b - 2 + jj < 32]

                    def expfn(kts_, sg, sres, pb, pres):
                        P.op("act", lambda e: e.activation(out=pb, in_=sg, func=AF.Exp, scale=0.125), reads=[sres], writes=[pres])

                    def extra_fn(kt, b=b, hb_=hb_):
                        jj = kt - (4 * b - 2)
                        e0 = (14 - 2 * jj) * 64
                        return idb[:, :], TH8[hb_][:, e0:e0 + 512], [("TH8", hb_), "idb"]

                    obk = obks[b % 2]

                    def fin(hb_=hb_, b=b, obk=obk):
                        self.finalize(obk[0], obk[1], rec, "recn", OST[0][hb_ * 64:hb_ * 64 + 64, b * TT:(b + 1) * TT], ("OSTn", 0, hb_), hb_ * 64)
                    self.pipe_block(pipe, kts,
                                    lambda kt, hb_=hb_: (KA[hb_][:, kt * 128:(kt + 1) * 128], ("KA", hb_, kt // 4)),
                                    QA[hb_][:, b * TT:(b + 1) * TT], [("QA", hb_, b), ("QAc", hb_), ("KAc", hb_)] + vres_all,
                                    lambda kt, h=h: (VAall[:, h, kt, :], ("VAn", kt)),
                                    obk[0], obk[1], expfn, fin, extra_fn)
            self.pipe_flush(pipe)
            P.dma("sp", "st_osn0", self.os_w[:, j], OST[0][:, :].rearrange("p (g t) -> p g t", g=8),
                  reads=[("OSTn", 0, 0), ("OSTn", 0, 1)])

    def mla_latent(self, H2, st, cqn, ckvn, KB, cosB, sinB):
        P, ps = self.P, self.ps
        P.dma("sp", "ld_tab", cosB[:, :], self.ROPE_B[0], writes=["cosB"])
        P.dma("sp", "ld_tab", sinB[:, :], self.ROPE_B[1], writes=["sinB"])
        wkr = self.sb(st, [128, 2 * 8 * 96], BF16, "wkr")
        P.dma("pool", "w_kr", wkr[:, :], self.WBKR, writes=["wkr"], max_dma_last_dim=4096)
        wcs = [self.sb(st, [128, 1024], BF16, "wcr") for _ in range(5)]
        for i in range(5):
            P.dma("pool", f"w_c{i}", wcs[i][:, :], self.WBC[i], writes=[("wcs", i)], max_dma_last_dim=4096)
        sets = []
        for k in range(2):
            sc = self.common_scratch(st)
            sc["rstd_res"] = ("rstd", k)
            sc["ss"] = (ps[4 * k + 3], ("ps", 4 * k + 3))
            sets.append(dict(sc=sc, pj=[(ps[4 * k + i], ("ps", 4 * k + i)) for i in range(3)],
                             t1=self.sb(st, [128, TT], F32, "t1m"), t2=self.sb(st, [128, TT], F32, "t2m")))
        for tt in range(8):
            tsl = slice(tt * TT, (tt + 1) * TT)
            hres = [("H2t", tt)]
            S = sets[tt % 2]
            pj, sc = S["pj"], S["sc"]
            for (c0, nch, dst, dname, gc0, dim) in ((0, 3, cqn, "cqn", 0, 384.0), (3, 2, ckvn, "ckvn", 3, 256.0)):
                for c in range(nch):
                    self.mm_group(pj[c][0][:, :], [(wcs[c0 + c][:, kc * 128:(kc + 1) * 128], H2[:, kc, tsl]) for kc in range(8)],
                                  reads=[("wcs", c0 + c)] + hres, writes=[pj[c][1]])
                self.norm_tile([pj[c][0][:, :] for c in range(nch)], [pj[c][1] for c in range(nch)],
                               [self.abg[:, gc0 + c:gc0 + c + 1] for c in range(nch)],
                               [dst[:, c, tsl] for c in range(nch)], [(dname, c, tt) for c in range(nch)],
                               dim, self.ones_bf[:, :], sc)
            self.mm_group(pj[0][0][0:96, :], [(wkr[:, kc * 96:(kc + 1) * 96], H2[:, kc, tsl]) for kc in range(8)],
                          reads=["wkr"] + hres, writes=[pj[0][1]])
            self.mm_group(pj[1][0][0:96, :], [(wkr[:, (8 + kc) * 96:(9 + kc) * 96], H2[:, kc, tsl]) for kc in range(8)],
                          reads=["wkr"] + hres, writes=[pj[1][1]])
            self.rope32(pj[0], pj[1], cosB, sinB, tsl, S["t1"], S["t2"], [KB[0][64:96, tsl], KB[1][64:96, tsl]],
                        [("KBr", 0, tt), ("KBr", 1, tt)], tag=tt % 2)

    def mla_attn(self, st, cqn, ckvn, KB, cosB, sinB):
        P, ps = self.P, self.ps
        wuq = self.sb(st, [128, 8 * 576], BF16, "wuq")
        for i in range(3):
            P.dma("pool", "w_uq", wuq[:, i * 1536:(i + 1) * 1536], self.WBUQ[:, i * 1536:(i + 1) * 1536], writes=["wuq"], max_dma_last_dim=4096)
        wukn = self.sb(st, [128, 1024], BF16, "wukn")
        P.dma("pool", "w_ukn", wukn[:, :], self.WBUKN, writes=["wukn"], max_dma_last_dim=4096)
        wuv = self.sb(st, [128, 1024], BF16, "wuv")
        P.dma("pool", "w_uv", wuv[:, :], self.WBUV, writes=["wuv"], max_dma_last_dim=4096)
        QB = [self.sb(st, [128, T], BF16, "QB") for _ in range(2)]
        VA = [self.sb(st, [128, 32, 128], BF16, "VAm") for _ in range(2)]
        VAx = [self.sb(st, [128, 32, 128], BF16, "VAmx") for _ in range(2)]
        for i in range(2):
            P.op("pool", lambda e, i=i: e.memset(VA[i][:, :, 64:128], 1.0), writes=[("VAm", i)])
        OST = [self.sb(st, [128, T], BF16, "OSTm") for _ in range(2)]
        pbufs = [(self.sb(st, [128, 2 * TT], BF16, "pbm"), ("pbm", i)) for i in range(3)]
        rec = self.sb(st, [128, TT], F32, "recm")
        t1 = self.sb(st, [128, TT], F32, "t1m")
        t2 = self.sb(st, [128, TT], F32, "t2m")
        cres = [("cqn", c, tt) for c in range(3) for tt in range(8)]
        kvres = [("ckvn", c, tt) for c in range(2) for tt in range(8)]
        sgroups = [(self.psall[:, 0:1024], ("psg", 0)), (self.psall[:, 1024:2048], ("psg", 1))]
        obks = [(ps[4], ("ps", 4)), (ps[5], ("ps", 5))]
        msets = [((ps[5], ("ps", 5)), (ps[6], ("ps", 6)), (ps[7], ("ps", 7))),
                 ((ps[0], ("psg", 0)), (ps[1], ("psg", 0)), (ps[2], ("psg", 1)))]
        t1s = [t1, self.sb(st, [128, TT], F32, "t1mb")]
        t2s = [t2, self.sb(st, [128, TT], F32, "t2mb")]
        scale = 96.0 ** -0.5
        pipe = self.pipe_new(sgroups, pbufs, 2)
        for h in range(8):
            hb_ = h % 2
            for tt in range(8):
                tsl = slice(tt * TT, (tt + 1) * TT)
                o0 = h * 576
                k_ = tt % 2
                pa, pb_, bkk = msets[k_]
                self.mm_group(pa[0][0:96, :], [(wuq[:, o0 + kc * 96:o0 + (kc + 1) * 96], cqn[:, kc, tsl]) for kc in range(3)],
                              reads=["wuq"] + [("cqn", c, tt) for c in range(3)], writes=[pa[1]])
                self.mm_group(pb_[0][0:96, :], [(wuq[:, o0 + 288 + kc * 96:o0 + 288 + (kc + 1) * 96], cqn[:, kc, tsl]) for kc in range(3)],
                              reads=["wuq"] + [("cqn", c, tt) for c in range(3)], writes=[pb_[1]])
                P.op("act", lambda e, tsl=tsl, hb_=hb_, pa=pa: e.activation(out=QB[hb_][0:64, tsl], in_=pa[0][0:64, :], func=AF.Copy),
                     reads=[pa[1]], writes=[("QB", hb_, tt)])
                self.rope32(pa, pb_, cosB, sinB, tsl, t1s[k_], t2s[k_], [QB[hb_][64:96, tsl]], [("QBr", hb_, tt)], tag=k_)
                bk, bkr = bkk
                self.mm_group(bk[0:64, :], [(wukn[:, (h * 2 + kc) * 64:(h * 2 + kc + 1) * 64], ckvn[:, kc, tsl]) for kc in range(2)],
                              reads=["wukn"] + [("ckvn", c, tt) for c in range(2)], writes=[bkr])
                P.op("dve", lambda e, bk=bk, tsl=tsl, hb_=hb_: e.tensor_copy(out=KB[hb_][0:64, tsl], in_=bk[0:64, :]),
                     reads=[bkr], writes=[("KB", hb_, tt)])
            self.vproj(None, 2, 64, lambda kc, tk: ckvn[:, kc, tk * 128:(tk + 1) * 128], kvres,
                       self._wuv_head(wuv, h), "wuv",
                       lambda: [(VA[hb_], ("VAm", hb_), 0)], Ring([(ps[5], ("ps", 5)), (ps[6], ("ps", 6))]))
            P.op("dve", lambda e, hb_=hb_: e.tensor_scalar(out=VAx[hb_][:, :, :], in0=VA[hb_][:, :, :], scalar1=self.maskT[:, 4:5], scalar2=None, op0=ALU.mult),
                 reads=[("VAm", hb_)], writes=[("VAmx", hb_)])
            for qb in range(8):
                qseg = qb // 4

                def expfn(kts, sg, sres, pb, pres):
                    P.op("act", lambda e: e.activation(out=pb, in_=sg, func=AF.Exp, scale=scale), reads=[sres], writes=[pres])

                def vfn(kt, hb_=hb_, qseg=qseg):
                    if kt // 16 != qseg:
                        return VAx[hb_][:, kt, :], ("VAmx", hb_)
                    return VA[hb_][:, kt, :], ("VAm", hb_)
                oi = (h // 2) % 2

                obk = obks[qb % 2]

                def fin(hb_=hb_, qb=qb, oi=oi, obk=obk):
                    self.finalize(obk[0], obk[1], rec, "recm", OST[oi][hb_ * 64:hb_ * 64 + 64, qb * TT:(qb + 1) * TT], ("OSTm", oi, hb_), hb_ * 64)
                self.pipe_block(pipe, list(range(32)),
                                lambda kt, hb_=hb_: (KB[hb_][0:96, kt * 128:(kt + 1) * 128], ("KB", hb_, kt // 4)),
                                QB[hb_][0:96, qb * TT:(qb + 1) * TT],
                                [("QB", hb_, qb), ("QBr", hb_, qb)] + [("KBr", hb_, t_) for t_ in range(8)],
                                vfn, obk[0], obk[1], expfn, fin)
            self.pipe_flush(pipe)
            if hb_ == 1:
                oi = (h // 2) % 2
                P.dma("sp", f"st_osm{oi}", self.os_w[:, 4 + h // 2], OST[oi][:, :].rearrange("p (g t) -> p g t", g=8),
                      reads=[("OSTm", oi, 0), ("OSTm", oi, 1)])

    def _wuv_head(self, wuv, h):
        return wuv[:, h * 128:(h + 1) * 128]

    def rope32(self, praw, pswp, cosB, sinB, tsl, t1, t2, dsts, dres, tag=0):
        P = self.P
        pr = slice(64, 96)
        n1, n2 = ("t1m", tag), ("t2m", tag)
        P.op("dve", lambda e: e.tensor_tensor(out=t1[pr, :], in0=praw[0][pr, :], in1=cosB[pr, tsl], op=ALU.mult),
             reads=[praw[1], "cosB"], writes=[n1])
        P.op("dve", lambda e: e.tensor_tensor(out=t2[pr, :], in0=pswp[0][pr, :], in1=sinB[pr, tsl], op=ALU.mult),
             reads=[pswp[1], "sinB"], writes=[n2])
        for d, r in zip(dsts, dres):
            P.op("pool", lambda e, d=d: e.tensor_tensor(out=d, in0=t1[pr, :], in1=t2[pr, :], op=ALU.add),
                 reads=[n1, n2], writes=[r])

    def mixer0(self):
        P = self.P
        with ExitStack() as st0:
            cqn = self.sb(st0, [128, 3, T], BF16, "cqn")
            ckvn = self.sb(st0, [128, 2, T], BF16, "ckvn")
            KB = [self.sb(st0, [128, T], BF16, "KB") for _ in range(2)]
            cosB = self.sb(st0, [128, T], F32, "cosB")
            sinB = self.sb(st0, [128, T], F32, "sinB")
            with ExitStack() as st:
                H2 = self.load_H2(st)
                self.mla_latent(H2, st, cqn, ckvn, KB, cosB, sinB)
                P.barrier()
                P.emit(st)
            with ExitStack() as st2:
                self.mla_attn(st2, cqn, ckvn, KB, cosB, sinB)
                P.barrier()
                P.emit(st2)
        with ExitStack() as st:
            H2 = self.load_H2(st)
            self.na_phase(H2, st)
            P.barrier()
            P.emit(st)

    def build(self):
        nc = self.nc
        dbg = self.debug
        self.X = self.din("x", [T, 1024])
        self.Y = nc.dram_tensor("y", [T, 1024], F32, kind="ExternalOutput").ap()
        self.WGU = self.din("wgu", [4, NFC, 128, 2048])
        self.WD = self.din("wd", [4, NFC, 128, 1024])
        self.WO = self.din("wo", [2, 8, 128, 1024])
        self.NGd = self.din("ng", [128, 56])
        self.IDd = self.din("ident", [128, 128])
        self.ONd = self.din("onesblk", [2, 128, 128])
        self.MKd = self.din("maskt", [128, 8])
        self.HGd = self.din("hg", [128, 4])
        self.ABGd = self.din("abg", [128, 5])
        self.ROPE_C = self.din("rope_c", [2, 128, T])
        self.ROPE_B = self.din("rope_b", [2, 128, T])
        self.WCQ = self.din("wcq", [8, 128, 2048])
        self.WCK = self.din("wck", [2, 128, 2048])
        self.WCV = self.din("wcv", [128, 2048])
        self.WAQK2 = self.din("waqk2", [4, 128, 2048])
        self.WAVA = self.din("wava", [128, 4096])
        self.NA_OH = self.din("na_oh", [64, T])
        self.NA_RM = self.din("na_rm", [64, T])
        self.NA_T = self.din("na_t", [8, 128, 22 * 64])
        self.WBC = self.din("wbc", [5, 128, 1024])
        self.WBKR = self.din("wbkr", [128, 1536])
        self.WBUQ = self.din("wbuq", [128, 4608])
        self.WBUKN = self.din("wbukn", [128, 1024])
        self.WBUV = self.din("wbuv", [128, 1024])
        kind = "ExternalOutput" if dbg else "Internal"
        self.xs_tm = nc.dram_tensor("xs", [8, 128, 8, TT], F32, kind=kind).ap()
        self.hs_tm = nc.dram_tensor("hs", [8, 128, 8, TT], BF16, kind=kind).ap()
        self.os_tm = nc.dram_tensor("os", [8, 128, 8, TT], BF16, kind=kind).ap()
        self.os_w = self.os_tm.rearrange("g p c t -> p c g t")
        with ExitStack() as gst:
            P = self.P = Prog(nc, gst)
            self.psall = gst.enter_context(nc.psum_tensor("psall", [128, 4096], F32))
            self.ps = [self.psall[:, i * 512:(i + 1) * 512] for i in range(8)]
            self.ng = self.sb(gst, [128, 56], F32, "ng")
            self.ident = self.sb(gst, [128, 128], F32, "ident")
            self.ones_bf = self.sb(gst, [128, 128], BF16, "ones")
            self.blk_bf = self.sb(gst, [128, 128], BF16, "blk")
            self.maskT = self.sb(gst, [128, 8], F32, "maskT")
            self.hg = self.sb(gst, [128, 4], F32, "hg")
            self.abg = self.sb(gst, [128, 5], F32, "abg")
            self.epsc = self.sb(gst, [128, 1], F32, "epsc")
            P.dma("sp", "ld_c", self.ng[:, :], self.NGd, writes=["c"])
            P.dma("sp", "ld_c", self.ident[:, :], self.IDd, writes=["c"])
            P.dma("sp", "ld_c", self.maskT[:, :], self.MKd, writes=["c"])
            P.dma("sp", "ld_c", self.hg[:, :], self.HGd, writes=["c"])
            P.dma("sp", "ld_c", self.abg[:, :], self.ABGd, writes=["c"])
            P.dma("pool", "ld_c2", self.ones_bf[:, :], self.ONd[0], writes=["c2"])
            P.dma("pool", "ld_c2", self.blk_bf[:, :], self.ONd[1], writes=["c2"])
            P.op("dve", lambda e: e.memset(self.epsc[:, :], EPS), writes=["c3"])
            P.barrier()
            stages = [lambda: self.front(0), self.mixer0, lambda: self.front(1), self.gqa_phase, lambda: self.front(2)]
            n = len(stages) if self.stop_after is None else self.stop_after
            for sfn in stages[:n]:
                sfn()
            with ExitStack() as st:
                P.barrier()
                P.emit(st)
        return nc


def _chunk_k(w, m):
    K, M = w.shape
    return np.ascontiguousarray(w.reshape(K // 128, 128, M).transpose(1, 0, 2).reshape(128, (K // 128) * M))


def _axial_angles(pos, rot_dim):
    row = (pos // 64).astype(np.float32)
    col = (pos % 64).astype(np.float32)
    axis_dim = rot_dim // 2
    inv = (np.float32(10000.0) ** (-np.arange(0, axis_dim, 2, dtype=np.float32) / np.float32(axis_dim))).astype(np.float32)
    return np.concatenate([row[:, None] * inv, col[:, None] * inv], axis=-1).astype(np.float32)


def _core_tables(is_prompt):
    u = np.arange(T)
    pos = u if is_prompt else (u % SEG)
    ang = _axial_angles(pos, 64)
    d = np.arange(128) % 64
    cosC = np.cos(ang)[:, d % 32].T.astype(np.float32)
    sinC = (np.sin(ang)[:, d % 32] * np.where(d < 32, -1.0, 1.0)[None, :]).T.astype(np.float32)
    rope_c = np.stack([cosC, sinC]).astype(np.float32)
    angb = _axial_angles(pos, 32)
    rope_b = np.zeros((2, 128, T), np.float32)
    dd = np.arange(32)
    rope_b[0, 64:96] = np.cos(angb)[:, dd % 16].T
    rope_b[1, 64:96] = (np.sin(angb)[:, dd % 16] * np.where(dd < 16, -1.0, 1.0)[None, :]).T
    maskt = np.zeros((128, 8), np.float32)
    maskt[:, 4] = 1.0
    if not is_prompt:
        maskt[:, 1] = NEG
        maskt[:, 2] = NEG
        maskt[:, 4] = 0.0
    lrow = u // 64
    oh = (np.arange(64)[:, None] == lrow[None, :]).astype(np.float32)
    if is_prompt:
        rs = np.clip(lrow - 4, 0, 56)
    else:
        base = (lrow // 32) * 32
        rs = np.clip((lrow - base) - 4, 0, 24) + base
    kr = np.arange(64)[:, None]
    rm = np.where((kr >= rs[None, :]) & (kr < rs[None, :] + 8), 0.0, NEG).astype(np.float32)
    return dict(rope_c=rope_c, rope_b=rope_b, maskt=maskt, na_oh=oh, na_rm=rm)


def _na_bias_table(bias):
    H = bias.shape[0]
    i = np.arange(2)[:, None, None, None]
    kc = np.arange(64)[None, :, None, None]
    e = np.arange(22)[None, None, :, None]
    qc = np.arange(64)[None, None, None, :]
    dr = 17 - e + i
    dc = kc - qc + 15
    cs = np.clip(qc - 8, 0, 48)
    valid = (dr >= 0) & (dr <= 14) & (kc >= cs) & (kc < cs + 16)
    valid = np.broadcast_to(valid, (2, 64, 22, 64))
    drc = np.broadcast_to(np.clip(dr, 0, 14), (2, 64, 22, 64))
    dcc = np.broadcast_to(np.clip(dc, 0, 30), (2, 64, 22, 64))
    out = np.empty((H, 2, 64, 22, 64), np.float32)
    for h in range(H):
        out[h] = np.where(valid, bias[h][drc, dcc], np.float32(NEG))
    return np.ascontiguousarray(out.reshape(H, 128, 22 * 64))


def _prep_shared(inp):
    f = lambda a: np.asarray(a, dtype=np.float32)
    sh = {}
    wg = [f(inp["ffn1_wg"])[0], f(inp["ffn2_wg"])[0], f(inp["ffn1_wg"])[1], f(inp["ffn2_wg"])[1]]
    wu = [f(inp["ffn1_wu"])[0], f(inp["ffn2_wu"])[0], f(inp["ffn1_wu"])[1], f(inp["ffn2_wu"])[1]]
    wd = [f(inp["ffn1_wd"])[0], f(inp["ffn2_wd"])[0], f(inp["ffn1_wd"])[1], f(inp["ffn2_wd"])[1]]
    FP = NFC * 128
    WGU = np.zeros((4, NFC, 128, 2, 8, 128), np.float32)
    WD = np.zeros((4, NFC, 128, 1024), np.float32)
    for i in range(4):
        g = np.zeros((1024, FP), np.float32); g[:, :2752] = wg[i]
        u_ = np.zeros((1024, FP), np.float32); u_[:, :2752] = wu[i]
        d_ = np.zeros((FP, 1024), np.float32); d_[:2752] = wd[i]
        WGU[i, :, :, 0] = g.reshape(8, 128, NFC, 128).transpose(2, 1, 0, 3)
        WGU[i, :, :, 1] = u_.reshape(8, 128, NFC, 128).transpose(2, 1, 0, 3)
        WD[i] = d_.reshape(NFC, 128, 1024)
    sh["wgu"] = WGU.reshape(4, NFC, 128, 2048)
    sh["wd"] = WD
    gl = [f(inp["norm_ffn1"])[0], f(inp["norm_mix"])[0], f(inp["norm_ffn2"])[0],
          f(inp["norm_ffn1"])[1], f(inp["norm_mix"])[1], f(inp["norm_ffn2"])[1], f(inp["final_norm"])]
    sh["ng"] = np.ascontiguousarray(np.concatenate([g.reshape(8, 128).T for g in gl], axis=1))
    sh["ident"] = np.eye(128, dtype=np.float32)
    blk = np.zeros((128, 128), np.float32); blk[:64, :64] = 1.0; blk[64:, 64:] = 1.0
    sh["onesblk"] = np.stack([np.ones((128, 128), np.float32), blk])
    w_in = f(inp["ab_w_in"])[0]
    qa, ka, va = w_in[:, 0:512], w_in[:, 512:1024], w_in[:, 1024:1536]
    cq, ckv, kr = w_in[:, 1536:1920], w_in[:, 1920:2176], w_in[:, 2176:2208]
    waqk2 = np.zeros((4, 128, 2, 8, 128), np.float32)
    for j in range(4):
        waqk2[j, :, 0] = qa[:, j * 128:(j + 1) * 128].reshape(8, 128, 128).transpose(1, 0, 2)
        waqk2[j, :, 1] = ka[:, j * 128:(j + 1) * 128].reshape(8, 128, 128).transpose(1, 0, 2)
    sh["waqk2"] = waqk2.reshape(4, 128, 2048)
    sh["wava"] = _chunk_k(va, 512)
    sh["na_t"] = _na_bias_table(f(inp["ab_na_bias"])[0])
    wbc = [_chunk_k(cq[:, c * 128:(c + 1) * 128], 128) for c in range(3)] + [_chunk_k(ckv[:, c * 128:(c + 1) * 128], 128) for c in range(2)]
    sh["wbc"] = np.stack(wbc)
    sw16 = (np.arange(32) + 16) % 32
    krp = np.zeros((2, 1024, 96), np.float32)
    krp[0, :, 64:] = kr
    krp[1, :, 64:] = kr[:, sw16]
    sh["wbkr"] = np.concatenate([_chunk_k(krp[0], 96), _chunk_k(krp[1], 96)], axis=1)
    wuq = f(inp["ab_w_uq"])[0]
    parts = []
    for h in range(8):
        wh = wuq[:, h * 96:(h + 1) * 96]
        whs = wh.copy()
        whs[:, 64:] = wh[:, 64:][:, sw16]
        parts += [_chunk_k(wh, 96), _chunk_k(whs, 96)]
    sh["wbuq"] = np.concatenate(parts, axis=1)
    wukv = f(inp["ab_w_ukv"])[0]
    sh["wbukn"] = np.concatenate([_chunk_k(wukv[:, h * 128:h * 128 + 64], 64) for h in range(8)], axis=1)
    sh["wbuv"] = np.concatenate([_chunk_k(wukv[:, h * 128 + 64:h * 128 + 128], 64) for h in range(8)], axis=1)
    qn, kvn = f(inp["ab_q_norm"])[0], f(inp["ab_kv_norm"])[0]
    sh["abg"] = np.ascontiguousarray(np.concatenate([qn.reshape(3, 128).T, kvn.reshape(2, 128).T], axis=1))
    wo0 = f(inp["ab_w_out"])[0]
    cw = f(inp["c_w_in"])[0]
    qw, kw, vw = cw[:, :1024], cw[:, 1024:1280], cw[:, 1280:1536]
    sw32 = (np.arange(64) + 32) % 64
    heads = []
    for qc in range(8):
        if qc < 4:
            heads.append((qc, 4 + qc))
        else:
            heads.append((8 + qc - 4, 12 + qc - 4))
    wcq = np.zeros((8, 128, 2, 8, 128), np.float32)
    rowidx = np.zeros((8, 128), np.int64)
    for qc, (ha, hb) in enumerate(heads):
        cols = np.concatenate([np.arange(ha * 64, ha * 64 + 64), np.arange(hb * 64, hb * 64 + 64)])
        colsw = np.concatenate([ha * 64 + sw32, hb * 64 + sw32])
        wcq[qc, :, 0] = qw[:, cols].reshape(8, 128, 128).transpose(1, 0, 2)
        wcq[qc, :, 1] = qw[:, colsw].reshape(8, 128, 128).transpose(1, 0, 2)
        rowidx[qc] = cols
    sh["wcq"] = wcq.reshape(8, 128, 2048)
    wck = np.zeros((2, 128, 2, 8, 128), np.float32)
    for kch in range(2):
        cols = np.arange(kch * 128, kch * 128 + 128)
        colsw = np.concatenate([kch * 128 + sw32, kch * 128 + 64 + sw32])
        wck[kch, :, 0] = kw[:, cols].reshape(8, 128, 128).transpose(1, 0, 2)
        wck[kch, :, 1] = kw[:, colsw].reshape(8, 128, 128).transpose(1, 0, 2)
    sh["wck"] = wck.reshape(2, 128, 2048)
    sh["wcv"] = _chunk_k(vw, 256)
    gq, gk = f(inp["c_q_norm"])[0], f(inp["c_k_norm"])[0]
    d = np.arange(128) % 64
    sh["hg"] = np.ascontiguousarray(np.stack([gq[d], gq[sw32[d]], gk[d], gk[sw32[d]]], axis=1))
    wo1 = f(inp["c_w_out"])[0][rowidx.reshape(-1)]
    WO = np.zeros((2, 8, 128, 8, 128), np.float32)
    for li, wo in enumerate((wo0, wo1)):
        WO[li] = wo.reshape(8, 128, 8, 128).transpose(2, 1, 0, 3)
    sh["wo"] = WO.reshape(2, 8, 128, 1024)
    return sh


_NC_CACHE = {}


def _get_nc(debug=False, stop_after=None):
    key = (debug, stop_after)
    if key not in _NC_CACHE:
        _NC_CACHE[key] = Builder(debug=debug, stop_after=stop_after).build()
    return _NC_CACHE[key]


def make_in_maps(inputs):
    sh = _prep_shared(inputs)
    xp = np.asarray(inputs["x_prompt"], dtype=np.float32)
    xsm = np.asarray(inputs["x_sample"], dtype=np.float32)
    tp, ts = _core_tables(True), _core_tables(False)
    maps = []
    for c in range(N_CORES):
        m = dict(sh)
        if c < 4:
            m["x"] = np.ascontiguousarray(xp[c])
            m.update(tp)
        else:
            m["x"] = np.ascontiguousarray(xsm[2 * (c - 4):2 * (c - 4) + 2].reshape(T, 1024))
            m.update(ts)
        maps.append(m)
    return maps


def kernel(**inputs):
    nc = _get_nc()
    maps = make_in_maps(inputs)
    res = run_bass_kernel_spmd(nc, maps, core_ids=list(range(N_CORES)))
    ys = [np.asarray(r["y"], dtype=np.float32) for r in res.results]
    y_prompt = np.stack(ys[:4]).reshape(4, 4096, 1024)
    y_sample = np.stack([y.reshape(2, 2048, 1024) for y in ys[4:]]).reshape(8, 2048, 1024)
    return (y_prompt, y_sample)
```

```python
import numpy as np
from contextlib import ExitStack
import concourse.bass as bass
import concourse.mybir as mybir
from concourse.bass_utils import run_bass_kernel_spmd

F32 = mybir.dt.float32
BF16 = mybir.dt.bfloat16
AF = mybir.ActivationFunctionType
ALU = mybir.AluOpType

T = 4096
SEG = 2048
TT = 512
NFC = 22
EPS = 1e-6
NEG = -30000.0
N_CORES = 8


class Prog:
    ENG = ("pe", "act", "dve", "pool", "sp")

    def __init__(self, nc, gstack, same_engine_sync=True):
        self.nc = nc
        self.gstack = gstack
        self.streams = {e: [] for e in self.ENG}
        self.count = {e: 0 for e in self.ENG}
        self.waited = {e: {} for e in self.ENG}
        self.res = {}
        self.dma_count = {}
        self.same_engine_sync = same_engine_sync
        self.sems = {}

    def _need(self, eng, reads, writes):
        need = {}

        def add(ev):
            if ev is None:
                return
            k, v = ev
            if k == eng and (eng == "pe" or not self.same_engine_sync):
                return
            if k in self.dma_count:
                v = self.dma_count[k]
            if need.get(k, 0) < v:
                need[k] = v

        for r in reads:
            st = self.res.get(r)
            if st:
                add(st[0])
        for w in writes:
            st = self.res.get(w)
            if st:
                add(st[0])
                for ev in st[1]:
                    add(ev)
        waits = []
        for k, v in need.items():
            if self.waited[eng].get(k, 0) >= v:
                continue
            self.waited[eng][k] = v
            waits.append((k, v))
        return waits

    def _record(self, ev, reads, writes):
        for r in reads:
            st = self.res.setdefault(r, [None, []])
            st[1].append(ev)
        for w in writes:
            self.res[w] = [ev, []]

    def op(self, eng, fn, reads=(), writes=(), sig=True):
        waits = self._need(eng, reads, writes)
        if sig:
            self.count[eng] += 1
            ev = (eng, self.count[eng])
            self._record(ev, reads, writes)
            self.streams[eng].append((waits, fn, (eng, 1), True))
        else:
            self.streams[eng].append((waits, fn, None, True))

    def dma(self, q, key, out, in_, reads=(), writes=(), **kw):
        waits = self._need(q, reads, writes)
        self.dma_count[key] = self.dma_count.get(key, 0) + 16
        ev = (key, self.dma_count[key])
        self._record(ev, reads, writes)

        def fn(e, out=out, in_=in_, kw=kw):
            return e.dma_start(out=out, in_=in_, **kw)
        self.streams[q].append((waits, fn, (key, 16), False))

    def barrier(self):
        keys = list(self.ENG) + list(self.dma_count.keys())
        for eng in self.ENG:
            waits = []
            for k in keys:
                v = self.dma_count.get(k, self.count.get(k, 0))
                if k == eng:
                    continue
                if v and self.waited[eng].get(k, 0) < v:
                    self.waited[eng][k] = v
                    waits.append((k, v))
            self.streams[eng].append((waits, None, None, False))
        self.res = {}

    def emit(self, stack):
        nc = self.nc
        for k in sorted(set(self.ENG) | set(self.dma_count.keys()), key=str):
            if k not in self.sems:
                self.sems[k] = self.gstack.enter_context(nc.semaphore("s_" + str(k)))
        block = stack.enter_context(nc.Block())
        sems = self.sems
        streams = self.streams
        self.streams = {e: [] for e in self.ENG}

        def run(e, stream):
            for waits, fn, inc, fuse in stream:
                fused = waits[-1] if (fuse and fn is not None and waits) else None
                for k, v in (waits[:-1] if fused else waits):
                    e.wait_ge(sems[k], v)
                if fn is None:
                    continue
                ins = fn(e)
                if fused is not None:
                    ins._wait_ge(sems[fused[0]], fused[1])
                if inc is not None:
                    ins.then_inc(sems[inc[0]], inc[1])

        @block.tensor
        def _(e):
            run(e, streams["pe"])

        @block.scalar
        def _(e):
            run(e, streams["act"])

        @block.vector
        def _(e):
            run(e, streams["dve"])

        @block.gpsimd
        def _(e):
            run(e, streams["pool"])

        @block.sync
        def _(e):
            run(e, streams["sp"])


class Ring:
    def __init__(self, items):
        self.items = items
        self.i = 0

    def next(self):
        it = self.items[self.i % len(self.items)]
        self.i += 1
        return it

    def peek(self, k=0):
        return self.items[(self.i + k) % len(self.items)]


class WStream:
    def __init__(self, P, name, slots, srcs, look=None, queue="pool"):
        self.P, self.name, self.slots, self.srcs = P, name, slots, srcs
        self.look = len(slots) - 1 if look is None else look
        self.issued = 0
        self.queue = queue

    def get(self, i):
        P = self.P
        hi = min(len(self.srcs), i + self.look + 1)
        while self.issued < hi:
            j = self.issued
            s = j % len(self.slots)
            P.dma(self.queue, f"w_{self.name}{s}", self.slots[s][:], self.srcs[j],
                  writes=[(self.name, s)], max_dma_last_dim=4096)
            self.issued += 1
        s = i % len(self.slots)
        return self.slots[s], (self.name, s)


class Builder:
    def __init__(self, debug=False, stop_after=None):
        self.debug = debug
        self.stop_after = stop_after
        self.nc = bass.Bass("TRN2", target_bir_lowering=False)
        self.uid = 0

    def din(self, name, shape, dt=F32):
        return self.nc.dram_tensor(name, list(shape), dt, kind="ExternalInput").ap()

    def sb(self, st, shape, dt, name=None):
        self.uid += 1
        return st.enter_context(self.nc.sbuf_tensor(f"{name or 't'}_{self.uid}", list(shape), dt))

    def pe_group(self, fns, reads, writes, pre=None):
        P = self.P
        n = len(fns)
        for i, fn in enumerate(fns):
            if i == n - 1 and n > 1:
                P.op("pe", fn, reads=reads, writes=writes, sig=True)
            elif i == 0:
                r0 = list(reads) + (list(pre[0]) if pre else [])
                w0 = list(writes) + (list(pre[1]) if pre else [])
                if n == 1:
                    waits = P._need("pe", r0, w0)
                    P.count["pe"] += 1
                    ev = ("pe", P.count["pe"])
                    P._record(ev, reads, writes)
                    P.streams["pe"].append((waits, fn, ("pe", 1), True))
                else:
                    P.op("pe", fn, reads=r0, writes=w0, sig=False)
            else:
                P.op("pe", fn, reads=(), writes=(), sig=False)

    def mm_group(self, out, pairs, reads, writes, pre=None):
        n = len(pairs)
        fns = []
        for i, (l, r) in enumerate(pairs):
            def fn(e, l=l, r=r, i=i):
                return e.matmul(out, lhsT=l, rhs=r, start=(i == 0), stop=(i == n - 1))
            fns.append(fn)
        self.pe_group(fns, reads, writes, pre=pre)

    def rstd_from_ss(self, ss_ps, ss_res, dim, lnv, rstd, rstd_res, parts=slice(0, 128)):
        P = self.P
        P.op("act", lambda e: e.activation(out=lnv[parts, :], in_=ss_ps[parts, :], func=AF.Ln,
                                           scale=1.0 / dim, bias=self.epsc[parts, 0:1]),
             reads=[ss_res], writes=[("lnv", id(lnv))])
        P.op("act", lambda e: e.activation(out=rstd[parts, :], in_=lnv[parts, :], func=AF.Exp, scale=-0.5),
             reads=[("lnv", id(lnv))], writes=[rstd_res])

    def norm_tile(self, srcs, src_res, gcols, dsts, dst_res, dim, lhsT, sc):
        P = self.P
        C = len(srcs)
        ssb, ssr = sc["ss"]
        for c in range(C):
            sq, sqr = sc["sq"].next()
            P.op("act", lambda e, c=c, sq=sq: e.activation(out=sq[:, :], in_=srcs[c], func=AF.Square),
                 reads=[src_res[c]], writes=[sqr])
            P.op("pe", lambda e, c=c, sq=sq: e.matmul(ssb[:, :], lhsT=lhsT, rhs=sq[:, :], start=(c == 0), stop=(c == C - 1)),
                 reads=[sqr], writes=[ssr])
        lnv, rstd, rstd_res = sc["lnv"], sc["rstd"], sc["rstd_res"]
        self.rstd_from_ss(ssb, ssr, dim, lnv, rstd, rstd_res)
        for c in range(C):
            P.op("dve", lambda e, c=c: e.scalar_tensor_tensor(out=dsts[c], in0=srcs[c], scalar=gcols[c], in1=rstd[:, :],
                                                               op0=ALU.mult, op1=ALU.mult),
                 reads=[src_res[c], rstd_res], writes=[dst_res[c]])

    def ffn_seg(self, fidx, normcol, xT, hb, ob, sc, wgu_s, wd_s, wbase):
        P = self.P
        psg, psu, psd = sc["psg"], sc["psu"], sc["psd"]
        for tt in range(4):
            tsl = slice(tt * TT, (tt + 1) * TT)
            self.norm_tile([xT[:, c, tsl] for c in range(8)], [("xT", c, tt) for c in range(8)],
                           [self.ng[:, normcol * 8 + c: normcol * 8 + c + 1] for c in range(8)],
                           [hb[:, c, tsl] for c in range(8)], [("hb", c, tt) for c in range(8)],
                           1024.0, self.ones_bf[:, :], sc)
        groups = [list(range(g, min(g + 4, NFC))) for g in range(0, NFC, 4)]
        for gi, grp in enumerate(groups):
            wds = []
            for fl, f in enumerate(grp):
                wslot, wres = wgu_s.get(wbase + f)
                dslot, dres = wd_s.get(wbase + f)
                wds.append((dslot, dres))
                ai = (gi % 2) * 4 + fl
                for tt in range(4):
                    tsl = slice(tt * TT, (tt + 1) * TT)
                    bg, bgr = psg.next()
                    bu, bur = psu.next()
                    hres = [("hb", c, tt) for c in range(8)]
                    self.mm_group(bg[:, :], [(wslot[:, kc * 128:(kc + 1) * 128], hb[:, kc, tsl]) for kc in range(8)],
                                  reads=[wres] + hres, writes=[bgr], pre=([], [bur]))
                    self.mm_group(bu[:, :], [(wslot[:, (8 + kc) * 128:(9 + kc) * 128], hb[:, kc, tsl]) for kc in range(8)],
                                  reads=[wres] + hres, writes=[bur], pre=([], [psg.peek()[1]]))
                    sg, sgr = sc["sg"].next()
                    P.op("act", lambda e, bg=bg, sg=sg: e.activation(out=sg[:, :], in_=bg[:, :], func=AF.Silu),
                         reads=[bgr], writes=[sgr])
                    P.op("dve", lambda e, sg=sg, bu=bu, ai=ai, tsl=tsl: e.tensor_tensor(out=ob[:, ai, tsl], in0=bu[:, :], in1=sg[:, :], op=ALU.mult),
                         reads=[bur, sgr], writes=[("ob", ai, tt)])
            for tt in range(4):
                for oc in range(8):
                    tsl = slice(tt * TT, (tt + 1) * TT)
                    if oc % 2 == 0:
                        bpair = [psd.next(), psd.next()]
                    bd, bdr = bpair[oc % 2]
                    pairs, rds = [], []
                    for fl, f in enumerate(grp):
                        ai = (gi % 2) * 4 + fl
                        pairs.append((wds[fl][0][:, oc * 128:(oc + 1) * 128], ob[:, ai, tsl]))
                        rds += [wds[fl][1], ("ob", ai, tt)]
                    self.mm_group(bd[:, :], pairs, reads=rds, writes=[bdr],
                                  pre=([], [bpair[1][1]]) if oc % 2 == 0 else ([], [psd.peek()[1]]))
                    P.op("dve", lambda e, bd=bd, oc=oc, tsl=tsl: e.scalar_tensor_tensor(
                        out=xT[:, oc, tsl], in0=bd[:, :], scalar=0.5, in1=xT[:, oc, tsl], op0=ALU.mult, op1=ALU.add),
                        reads=[bdr, ("xT", oc, tt)], writes=[("xT", oc, tt)])

    def front(self, fi):
        P, nc = self.P, self.nc
        first, last = fi == 0, fi == 2
        with ExitStack() as st:
            xT = self.sb(st, [128, 8, SEG], F32, "xT")
            hb = self.sb(st, [128, 8, SEG], BF16, "hb")
            ob = self.sb(st, [128, 8, SEG], BF16, "ob")
            stg = [self.sb(st, [128, 1024], F32, "stg") for _ in range(2)]
            sc = self.common_scratch(st)
            sc["sg"] = Ring([(self.sb(st, [128, TT], F32, "sg"), ("sg", i)) for i in range(2)])
            ps = self.ps
            sc["psg"] = Ring([(ps[0], ("ps", 0)), (ps[1], ("ps", 1))])
            sc["psu"] = Ring([(ps[2], ("ps", 2)), (ps[3], ("ps", 3))])
            sc["psd"] = Ring([(ps[4], ("ps", 4)), (ps[5], ("ps", 5)), (ps[6], ("ps", 6)), (ps[7], ("ps", 7))])
            sc["ss"] = (ps[6], ("ps", 6))
            pst = Ring([(ps[4], ("ps", 4)), (ps[5], ("ps", 5)), (ps[7], ("ps", 7))])
            ffns = []
            if not first:
                ffns.append(2 * (fi - 1) + 1)
            if not last:
                ffns.append(2 * fi)
            srcs_gu, srcs_d = [], []
            for s in range(2):
                for f_ in ffns:
                    for f in range(NFC):
                        srcs_gu.append(self.WGU[f_, f])
                        srcs_d.append(self.WD[f_, f])
            wgu_s = WStream(P, "wgu", [self.sb(st, [128, 2048], BF16, "wgu") for _ in range(3)], srcs_gu)
            wd_s = WStream(P, "wd", [self.sb(st, [128, 1024], BF16, "wd") for _ in range(8)], srcs_d, look=4)
            if not first:
                wo_t = [self.sb(st, [128, 1024], BF16, "wo") for _ in range(8)]
                for oc in range(8):
                    P.dma("pool", f"w_wo{oc}", wo_t[oc][:, :], self.WO[fi - 1, oc], writes=[("wo", oc)], max_dma_last_dim=4096)
            wpos = 0
            for s in range(2):
                t0 = s * SEG
                if first:
                    for tk in range(16):
                        buf = stg[tk % 2]
                        P.dma("sp", f"stg{tk % 2}", buf[:, :], self.X[t0 + tk * 128: t0 + (tk + 1) * 128, :],
                              writes=[("stg", tk % 2)])
                        for half in range(2):
                            bank, br = pst.next()
                            fns = []
                            for q in range(4):
                                fc = half * 4 + q
                                fns.append(lambda e, bank=bank, q=q, fc=fc, buf=buf: e.transpose(
                                    out=bank[:, q * 128:(q + 1) * 128], in_=buf[:, fc * 128:(fc + 1) * 128], identity=self.ident[:, :]))
                            self.pe_group(fns, reads=[("stg", tk % 2)], writes=[br], pre=([], [pst.peek()[1]]))
                            eng = "act" if half == 0 else "dve"
                            dst = xT[:, half * 4:(half + 1) * 4, tk * 128:(tk + 1) * 128]
                            src = bank[:, :].rearrange("p (a b) -> p a b", a=4)
                            wr = [("xT", half * 4 + q, tk // 4) for q in range(4)]
                            if eng == "act":
                                P.op("act", lambda e, dst=dst, src=src: e.activation(out=dst, in_=src, func=AF.Copy), reads=[br], writes=wr)
                            else:
                                P.op("dve", lambda e, dst=dst, src=src: e.tensor_copy(out=dst, in_=src), reads=[br], writes=wr)
                else:
                    for tt in range(4):
                        tsl = slice(tt * TT, (tt + 1) * TT)
                        g = s * 4 + tt
                        P.dma("sp", f"ld_ob{tt}", ob[:, :, tsl], self.os_tm[g], writes=[("ob", c, tt) for c in range(8)])
                        P.dma("sp", f"ld_xT{tt}", xT[:, :, tsl], self.xs_tm[g], writes=[("xT", c, tt) for c in range(8)])
                    for tt in range(4):
                        for oc in range(8):
                            wslot, wres = wo_t[oc], ("wo", oc)
                            tsl = slice(tt * TT, (tt + 1) * TT)
                            bd, bdr = sc["psd"].next()
                            self.mm_group(bd[:, :], [(wslot[:, kc * 128:(kc + 1) * 128], ob[:, kc, tsl]) for kc in range(8)],
                                          reads=[wres] + [("ob", kc, tt) for kc in range(8)], writes=[bdr],
                                          pre=([], [sc["psd"].peek()[1]]))
                            P.op("dve", lambda e, bd=bd, oc=oc, tsl=tsl: e.tensor_tensor(
                                out=xT[:, oc, tsl], in0=bd[:, :], in1=xT[:, oc, tsl], op=ALU.add),
                                reads=[bdr, ("xT", oc, tt)], writes=[("xT", oc, tt)])
                    self.ffn_seg(ffns[0], 3 * (fi - 1) + 2, xT, hb, ob, sc, wgu_s, wd_s, wpos)
                    wpos += NFC
                if not last:
                    self.ffn_seg(ffns[-1], 3 * fi, xT, hb, ob, sc, wgu_s, wd_s, wpos)
                    wpos += NFC
                    for tt in range(4):
                        tsl = slice(tt * TT, (tt + 1) * TT)
                        P.dma("sp", f"st_xs{tt}", self.xs_tm[s * 4 + tt], xT[:, :, tsl], reads=[("xT", c, tt) for c in range(8)])
                    for tt in range(4):
                        tsl = slice(tt * TT, (tt + 1) * TT)
                        self.norm_tile([xT[:, c, tsl] for c in range(8)], [("xT", c, tt) for c in range(8)],
                                       [self.ng[:, (3 * fi + 1) * 8 + c:(3 * fi + 1) * 8 + c + 1] for c in range(8)],
                                       [hb[:, c, tsl] for c in range(8)], [("hb", c, tt) for c in range(8)],
                                       1024.0, self.ones_bf[:, :], sc)
                        g = s * 4 + tt
                        P.dma("sp", f"st_hs{tt}", self.hs_tm[g], hb[:, :, tsl], reads=[("hb", c, tt) for c in range(8)])
                else:
                    for tt in range(4):
                        tsl = slice(tt * TT, (tt + 1) * TT)
                        self.norm_tile([xT[:, c, tsl] for c in range(8)], [("xT", c, tt) for c in range(8)],
                                       [self.ng[:, 48 + c:48 + c + 1] for c in range(8)],
                                       [xT[:, c, tsl] for c in range(8)], [("xT", c, tt) for c in range(8)],
                                       1024.0, self.ones_bf[:, :], sc)
                    for tk in range(16):
                        buf = stg[tk % 2]
                        for half in range(2):
                            bank, br = pst.next()
                            fns = []
                            for q in range(4):
                                fc = half * 4 + q
                                fns.append(lambda e, bank=bank, q=q, fc=fc, tk=tk: e.transpose(
                                    out=bank[:, q * 128:(q + 1) * 128], in_=xT[:, fc, tk * 128:(tk + 1) * 128], identity=self.ident[:, :]))
                            self.pe_group(fns, reads=[("xT", half * 4 + qq, tk // 4) for qq in range(4)], writes=[br], pre=([], [pst.peek()[1]]))
                            dst = buf[:, half * 512:(half + 1) * 512]
                            if half == 0:
                                P.op("act", lambda e, dst=dst, bank=bank: e.activation(out=dst, in_=bank[:, :], func=AF.Copy),
                                     reads=[br], writes=[("stg", tk % 2, 0)])
                            else:
                                P.op("dve", lambda e, dst=dst, bank=bank: e.tensor_copy(out=dst, in_=bank[:, :]),
                                     reads=[br], writes=[("stg", tk % 2, 1)])
                        P.dma("sp", f"st_y{tk % 2}", self.Y[t0 + tk * 128: t0 + (tk + 1) * 128, :], buf[:, :],
                              reads=[("stg", tk % 2, 0), ("stg", tk % 2, 1)])
            P.barrier()
            P.emit(st)

    def _touch_read(self, eng, reads):
        P = self.P
        ev = (eng, P.count[eng])
        for r in reads:
            stt = P.res.setdefault(r, [None, []])
            stt[1].append(ev)

    def common_scratch(self, st):
        sc = {}
        sc["sq"] = Ring([(self.sb(st, [128, TT], BF16, "sq"), ("sq", i)) for i in range(2)])
        sc["lnv"] = self.sb(st, [128, TT], F32, "lnv")
        sc["rstd"] = self.sb(st, [128, TT], F32, "rstd")
        sc["rstd_res"] = ("rstd", 0)
        return sc

    def attend(self, kts, kfn, qrhs, qres, vfn, obank, obr, sbanks, pbufs, expfn):
        P = self.P
        n = len(kts)
        for i in range(n + 2):
            if i < n:
                kt = kts[i]
                sbk, sres = sbanks[i % len(sbanks)]
                kl, kres = kfn(kt)
                P.op("pe", lambda e, sbk=sbk, kl=kl: e.matmul(sbk[:, :], lhsT=kl, rhs=qrhs, start=True, stop=True),
                     reads=[kres] + list(qres), writes=[sres])
            j = i - 1
            if 0 <= j < n:
                sbk, sres = sbanks[j % len(sbanks)]
                pb, pres = pbufs[j % len(pbufs)]
                expfn(j, kts[j], sbk, sres, pb, pres)
            j = i - 2
            if 0 <= j < n:
                pb, pres = pbufs[j % len(pbufs)]
                vl, vres = vfn(kts[j])
                P.op("pe", lambda e, vl=vl, pb=pb, j=j: e.matmul(obank[:, :], lhsT=vl, rhs=pb[:, :], start=(j == 0), stop=(j == n - 1)),
                     reads=[vres, pres], writes=[obr])

    def attend2(self, kts, kfn, qrhs, qres, vfn, obank, obr, sgroups, pbufs, G, expfn, extra_fn=None):
        P = self.P
        groups = [kts[i:i + G] for i in range(0, len(kts), G)]
        n = len(groups)
        for i in range(n + 2):
            if i < n:
                sg, sres = sgroups[i % len(sgroups)]
                for j, kt in enumerate(groups[i]):
                    kl, kres = kfn(kt)
                    if extra_fn is None:
                        P.op("pe", lambda e, sg=sg, kl=kl, j=j: e.matmul(sg[:, j * 512:(j + 1) * 512], lhsT=kl, rhs=qrhs, start=True, stop=True),
                             reads=[kres] + list(qres), writes=[sres])
                    else:
                        xl, xr, xres = extra_fn(kt)
                        P.op("pe", lambda e, sg=sg, kl=kl, j=j: e.matmul(sg[:, j * 512:(j + 1) * 512], lhsT=kl, rhs=qrhs, start=True, stop=False),
                             reads=[kres] + list(qres) + list(xres), writes=[sres], sig=False)
                        P.op("pe", lambda e, sg=sg, xl=xl, xr=xr, j=j: e.matmul(sg[:, j * 512:(j + 1) * 512], lhsT=xl, rhs=xr, start=False, stop=True),
                             reads=[kres] + list(qres) + list(xres), writes=[sres])
            j = i - 1
            if 0 <= j < n:
                sg, sres = sgroups[j % len(sgroups)]
                pb, pres = pbufs[j % len(pbufs)]
                w = len(groups[j]) * 512
                expfn(j, groups[j], sg[:, 0:w], sres, pb[:, 0:w], pres)
            j = i - 2
            if 0 <= j < n:
                pb, pres = pbufs[j % len(pbufs)]
                for jj, kt in enumerate(groups[j]):
                    vl, vres = vfn(kt)
                    first = (j == 0 and jj == 0)
                    last = (j == n - 1 and jj == len(groups[j]) - 1)
                    P.op("pe", lambda e, vl=vl, pb=pb, jj=jj, first=first, last=last: e.matmul(
                        obank[:, :], lhsT=vl, rhs=pb[:, jj * 512:(jj + 1) * 512], start=first, stop=last),
                        reads=[vres, pres], writes=[obr])

    def pipe_new(self, sgroups, pbufs, G):
        return {"sg": sgroups, "pb": pbufs, "G": G, "cnt": 0, "pend": []}

    def _pipe_S(self, it):
        P = self.P
        sg, sres = it["sg"]
        for j, kt in enumerate(it["grp"]):
            kl, kres = it["kfn"](kt)
            qrhs, qres = it["qrhs"], it["qres"]
            if it["extra_fn"] is None:
                P.op("pe", lambda e, sg=sg, kl=kl, j=j, qrhs=qrhs: e.matmul(sg[:, j * 512:(j + 1) * 512], lhsT=kl, rhs=qrhs, start=True, stop=True),
                     reads=[kres] + list(qres), writes=[sres])
            else:
                xl, xr, xres = it["extra_fn"](kt)
                P.op("pe", lambda e, sg=sg, kl=kl, j=j, qrhs=qrhs: e.matmul(sg[:, j * 512:(j + 1) * 512], lhsT=kl, rhs=qrhs, start=True, stop=False),
                     reads=[kres] + list(qres) + list(xres), writes=[sres], sig=False)
                P.op("pe", lambda e, sg=sg, xl=xl, xr=xr, j=j: e.matmul(sg[:, j * 512:(j + 1) * 512], lhsT=xl, rhs=xr, start=False, stop=True),
                     reads=[kres] + list(qres) + list(xres), writes=[sres])

    def _pipe_E(self, it):
        if it["exped"]:
            return
        it["exped"] = True
        sg, sres = it["sg"]
        pb, pres = it["pb"]
        w = len(it["grp"]) * 512
        it["expfn"](it["grp"], sg[:, 0:w], sres, pb[:, 0:w], pres)

    def _pipe_V(self, it):
        P = self.P
        pb, pres = it["pb"]
        obank, obr = it["obank"], it["obr"]
        n = len(it["grp"])
        for jj, kt in enumerate(it["grp"]):
            vl, vres = it["vfn"](kt)
            first = it["first"] and jj == 0
            last = it["last"] and jj == n - 1
            P.op("pe", lambda e, vl=vl, pb=pb, jj=jj, first=first, last=last, obank=obank: e.matmul(
                obank[:, :], lhsT=vl, rhs=pb[:, jj * 512:(jj + 1) * 512], start=first, stop=last),
                reads=[vres, pres], writes=[obr])
        if it["last"] and it["fin"] is not None:
            it["fin"]()

    def pipe_block(self, pipe, kts, kfn, qrhs, qres, vfn, obank, obr, expfn, fin, extra_fn=None):
        G = pipe["G"]
        groups = [kts[i:i + G] for i in range(0, len(kts), G)]
        for gi, grp in enumerate(groups):
            idx = pipe["cnt"]
            pipe["cnt"] += 1
            it = dict(grp=grp, kfn=kfn, qrhs=qrhs, qres=qres, vfn=vfn, obank=obank, obr=obr, expfn=expfn, extra_fn=extra_fn,
                      first=(gi == 0), last=(gi == len(groups) - 1), fin=fin, exped=False,
                      sg=pipe["sg"][idx % len(pipe["sg"])], pb=pipe["pb"][idx % len(pipe["pb"])])
            self._pipe_S(it)
            pend = pipe["pend"]
            pend.append(it)
            if len(pend) >= 2:
                self._pipe_E(pend[-2])
            if len(pend) >= 3:
                self._pipe_V(pend.pop(0))

    def pipe_flush(self, pipe):
        for it in pipe["pend"]:
            self._pipe_E(it)
        for it in pipe["pend"]:
            self._pipe_V(it)
        pipe["pend"] = []

    def finalize(self, obank, obr, rec, recr, dst, dres, base):
        P = self.P
        pr = slice(base, base + 64)
        P.op("dve", lambda e: e.reciprocal(out=rec[pr, :], in_=obank[64:128, :]), reads=[obr], writes=[recr])
        P.op("dve", lambda e: e.tensor_tensor(out=dst, in0=obank[0:64, :], in1=rec[pr, :], op=ALU.mult),
             reads=[obr, recr], writes=[dres])

    def load_H2(self, st):
        P = self.P
        H2 = self.sb(st, [128, 8, T], BF16, "H2")
        for tt in range(8):
            tsl = slice(tt * TT, (tt + 1) * TT)
            P.dma("sp", f"ld_H2_{tt}", H2[:, :, tsl], self.hs_tm[tt], writes=[("H2t", tt)])
        return H2

    def vproj(self, VA, nk, ncols, lhs_fn, lhs_res, wv, wvres, col_fn, banks):
        P = self.P
        per = TT // ncols
        cnt = 0
        for g in range(0, 32, per):
            bank, br = banks.next()
            for i in range(per):
                tk = g + i
                self.mm_group(bank[:, i * ncols:(i + 1) * ncols],
                              [(lhs_fn(kc, tk), wv[:, kc * ncols:(kc + 1) * ncols]) for kc in range(nk)],
                              reads=[wvres] + lhs_res, writes=[br])
            src3 = bank[:, :].rearrange("p (a b) -> p a b", a=per)
            cnt += 1
            for (va, vres, c0) in col_fn():
                eng = "act" if cnt % 2 == 0 else "dve"
                dst = va[:, g:g + per, 0:64]
                src = src3[:, :, c0:c0 + 64]
                if eng == "act":
                    P.op("act", lambda e, dst=dst, src=src: e.activation(out=dst, in_=src, func=AF.Copy), reads=[br], writes=[vres])
                else:
                    P.op("dve", lambda e, dst=dst, src=src: e.tensor_copy(out=dst, in_=src), reads=[br], writes=[vres])

    def gqa_phase(self):
        P, ps = self.P, self.ps
        with ExitStack() as st:
            ctab = [self.sb(st, [128, 2, TT], F32, "ctab") for _ in range(2)]
            ccnt = [0]

            def load_tab(tt):
                i = ccnt[0] % 2
                ccnt[0] += 1
                P.dma("sp", f"ld_ct{i}", ctab[i][:, :, :], self.ROPE_C.rearrange("a p t -> p a t")[:, :, tt * TT:(tt + 1) * TT],
                      writes=[("ctab", i)])
                return ctab[i], ("ctab", i)

            KT2 = [self.sb(st, [128, T], BF16, "KT2") for _ in range(2)]
            wk = [self.sb(st, [128, 2048], BF16, "wk") for _ in range(2)]
            for i in range(2):
                P.dma("pool", f"w_wk{i}", wk[i][:, :], self.WCK[i], writes=[("wk", i)], max_dma_last_dim=4096)
            wv = self.sb(st, [128, 2048], BF16, "wv")
            P.dma("pool", "w_wv", wv[:, :], self.WCV, writes=["wv"], max_dma_last_dim=4096)
            VA = [self.sb(st, [128, 32, 128], BF16, "VA") for _ in range(4)]
            VAx = [self.sb(st, [128, 32, 128], BF16, "VAx") for _ in range(4)]
            for i in range(4):
                P.op("pool", lambda e, i=i: e.memset(VA[i][:, :, 64:128], 1.0), writes=[("VA", i)])
            sc = self.common_scratch(st)
            QT = [self.sb(st, [128, T], BF16, "QT") for _ in range(2)]
            OSTh = [self.sb(st, [128, T], BF16, "OSTh") for i in range(2)]
            pbufs = [(self.sb(st, [128, 2 * TT], BF16, "pb"), ("pb", i)) for i in range(3)]
            rec = [self.sb(st, [128, TT], F32, "rec") for _ in range(2)]
            tlo = [self.sb(st, [128, TT], F32, "tlo") for _ in range(2)]
            thi = [self.sb(st, [128, TT], F32, "thi") for _ in range(2)]
            ssum = [self.sb(st, [128, TT], F32, "ssum") for _ in range(2)]
            h2r = [self.sb(st, [128, 8, TT], BF16, "h2t") for _ in range(2)]
            h2cnt = [0]

            def load_h2(tt):
                i = h2cnt[0] % 2
                h2cnt[0] += 1
                P.dma("sp", f"ld_h2t{i}", h2r[i][:, :, :], self.hs_tm[tt], writes=[("h2t", i)])
                return h2r[i], ("h2t", i)

            wq_s = WStream(P, "wq", [self.sb(st, [128, 2048], BF16, "wq") for _ in range(2)], [self.WCQ[i] for i in range(8)])
            sgroups = [(self.psall[:, 0:1024], ("psg", 0)), (self.psall[:, 1024:2048], ("psg", 1))]
            obanks = [(ps[4], ("ps", 4)), (ps[5], ("ps", 5))]
            praw, pswp = (ps[6], ("ps", 6)), (ps[7], ("ps", 7))

            psets = [((ps[4], ("ps", 4)), (ps[5], ("ps", 5))), ((ps[6], ("ps", 6)), (ps[7], ("ps", 7)))]
            scr = []
            for i in range(2):
                scr.append(dict(t1=self.sb(st, [128, TT], F32, "t1"), t2=self.sb(st, [128, TT], F32, "t2"), t3=self.sb(st, [128, TT], F32, "t3"),
                                lnv=self.sb(st, [128, TT], F32, "lnv"), rstd=self.sb(st, [128, TT], F32, "rstd")))
            pcnt = [0]

            def proj_rope(wslot, wres, h2t, h2res, tt, gc, gsc, dsts, ct, ctres):
                k = pcnt[0] % 2
                pcnt[0] += 1
                praw, pswp = psets[k]
                S = scr[k]
                t1, t2, t3, lnv, rstd = S["t1"], S["t2"], S["t3"], S["lnv"], S["rstd"]
                n1, n2, n3, nr = ("t1", k), ("t2", k), ("t3", k), ("rstdq", k)
                self.mm_group(praw[0][:, :], [(wslot[:, kc * 128:(kc + 1) * 128], h2t[:, kc, :]) for kc in range(8)],
                              reads=[wres, h2res], writes=[praw[1]], pre=([], [pswp[1]]))
                self.mm_group(pswp[0][:, :], [(wslot[:, (8 + kc) * 128:(9 + kc) * 128], h2t[:, kc, :]) for kc in range(8)],
                              reads=[wres, h2res], writes=[pswp[1]])
                sq, sqr = sc["sq"].next()
                P.op("act", lambda e: e.activation(out=sq[:, :], in_=praw[0][:, :], func=AF.Square), reads=[praw[1]], writes=[sqr])
                P.op("dve", lambda e: e.scalar_tensor_tensor(out=t1[:, :], in0=praw[0][:, :], scalar=self.hg[:, gc:gc + 1], in1=ct[:, 0, :],
                                                              op0=ALU.mult, op1=ALU.mult), reads=[praw[1], ctres, sqr], writes=[n1])
                P.op("dve", lambda e: e.scalar_tensor_tensor(out=t2[:, :], in0=pswp[0][:, :], scalar=self.hg[:, gsc:gsc + 1], in1=ct[:, 1, :],
                                                              op0=ALU.mult, op1=ALU.mult), reads=[pswp[1], ctres], writes=[n2])
                P.op("pe", lambda e: e.matmul(praw[0][:, :], lhsT=self.blk_bf[:, :], rhs=sq[:, :], start=True, stop=True),
                     reads=[sqr, n1], writes=[praw[1]])
                self.rstd_from_ss(praw[0], praw[1], 64.0, lnv, rstd, nr)
                P.op("pool", lambda e: e.tensor_tensor(out=t3[:, :], in0=t1[:, :], in1=t2[:, :], op=ALU.add), reads=[n1, n2], writes=[n3])
                for (dst, pr, dres) in dsts:
                    P.op("pool", lambda e, dst=dst, pr=pr: e.tensor_tensor(out=dst, in0=t3[pr, :], in1=rstd[pr, :], op=ALU.mult),
                         reads=[n3, nr], writes=[dres])

            vb = Ring([(ps[4], ("ps", 4)), (ps[5], ("ps", 5))])
            vcnt = 0
            for tt in range(8):
                tsl = slice(tt * TT, (tt + 1) * TT)
                h2t, h2res = load_h2(tt)
                ct, ctres = load_tab(tt)
                for kch in range(2):
                    proj_rope(wk[kch], ("wk", kch), h2t, h2res, tt, 2, 3,
                              [(KT2[kch][:, tsl], slice(0, 128), ("KTd", kch, tt))], ct, ctres)
                for half in range(2):
                    bank, br = vb.next()
                    for i in range(2):
                        tk = half * 2 + i
                        self.mm_group(bank[:, i * 256:(i + 1) * 256],
                                      [(h2t[:, kc, tk * 128:(tk + 1) * 128], wv[:, kc * 256:(kc + 1) * 256]) for kc in range(8)],
                                      reads=["wv", h2res], writes=[br])
                    src3 = bank[:, :].rearrange("p (a b) -> p a b", a=2)
                    vcnt += 1
                    for kv in range(4):
                        g = tt * 4 + half * 2
                        dst = VA[kv][:, g:g + 2, 0:64]
                        src = src3[:, :, kv * 64:(kv + 1) * 64]
                        if vcnt % 2 == 0:
                            P.op("act", lambda e, dst=dst, src=src: e.activation(out=dst, in_=src, func=AF.Copy), reads=[br], writes=[("VA", kv)])
                        else:
                            P.op("dve", lambda e, dst=dst, src=src: e.tensor_copy(out=dst, in_=src), reads=[br], writes=[("VA", kv)])
            for kv in range(4):
                P.op("dve", lambda e, kv=kv: e.tensor_scalar(out=VAx[kv][:, :, :], in0=VA[kv][:, :, :], scalar1=self.maskT[:, 4:5], scalar2=None, op0=ALU.mult),
                     reads=[("VA", kv)], writes=[("VAx", kv)])
            icnt = [0]

            def plan_q(qc, tt):
                wslot, wres = wq_s.get(qc)
                h2t, h2res = load_h2(tt)
                ct, ctres = load_tab(tt)
                proj_rope(wslot, wres, h2t, h2res, tt, 0, 1,
                          [(QT[qc % 2][:, tt * TT:(tt + 1) * TT], slice(0, 128), ("QT", qc % 2, tt))], ct, ctres)

            scale = 64.0 ** -0.5
            lvl = 3
            for tt in range(8 if lvl >= 2 else 0):
                plan_q(0, tt)
            for qc in range(8 if lvl >= 3 else 0):
                kch = qc // 4
                noint = 1
                if noint and qc + 1 < 8:
                    for tt in range(8):
                        plan_q(qc + 1, tt)
                obk4 = [(ps[4 + i], ("ps", 4 + i)) for i in range(4)]
                items = [(tt, kt) for tt in range(8) for kt in range(32)]
                pend = []

                def stage_S(it):
                    tt, kt, idx = it
                    sg, sres = sgroups[idx % 2]
                    qt = QT[qc % 2]
                    for hh in range(2):
                        pr = slice(hh * 64, hh * 64 + 64)
                        P.op("pe", lambda e, sg=sg, hh=hh, pr=pr, kt=kt, tt=tt, qt=qt, kch=kch: e.matmul(
                            sg[:, hh * 512:(hh + 1) * 512], lhsT=KT2[kch][pr, kt * 128:(kt + 1) * 128], rhs=qt[pr, tt * TT:(tt + 1) * TT],
                            start=True, stop=True),
                            reads=[("KTd", kch, kt // 4), ("QT", qc % 2, tt)], writes=[sres])

                def stage_E(it):
                    tt, kt, idx = it
                    sg, sres = sgroups[idx % 2]
                    pb, pres = pbufs[idx % 3]
                    P.op("act", lambda e, sg=sg, pb=pb: e.activation(out=pb[:, :], in_=sg, func=AF.Exp, scale=scale), reads=[sres], writes=[pres])

                def stage_V(it):
                    tt, kt, idx = it
                    pb, pres = pbufs[idx % 3]
                    qseg = tt // 4
                    for hh in range(2):
                        kv = 2 * kch + hh
                        if kt // 16 != qseg:
                            va, vres = VAx[kv], ("VAx", kv)
                        else:
                            va, vres = VA[kv], ("VA", kv)
                        for half in range(2):
                            ob_, obr = obk4[hh * 2 + half]
                            pr = slice(half * 64, half * 64 + 64)
                            P.op("pe", lambda e, ob_=ob_, va=va, pr=pr, kt=kt, pb=pb, hh=hh: e.matmul(
                                ob_[:, :], lhsT=va[pr, kt, :], rhs=pb[pr, hh * 512:(hh + 1) * 512], start=(kt == 0), stop=(kt == 31)),
                                reads=[vres, pres], writes=[obr])
                    if kt == 31:
                        for hh in range(2):
                            lo, lor = obk4[hh * 2]
                            P.op("dve", lambda e, hh=hh, lo=lo: e.tensor_copy(out=tlo[hh][:, :], in_=lo[:, :]), reads=[lor], writes=[("tlo", hh)])
                        for hh in range(2):
                            hi, hir = obk4[hh * 2 + 1]
                            P.op("act", lambda e, hh=hh, hi=hi: e.activation(out=thi[hh][:, :], in_=hi[:, :], func=AF.Copy), reads=[hir], writes=[("thi", hh)])
                        for hh in range(2):
                            P.op("pool", lambda e, hh=hh: e.tensor_tensor(out=ssum[hh][:, :], in0=tlo[hh][:, :], in1=thi[hh][:, :], op=ALU.add),
                                 reads=[("tlo", hh), ("thi", hh)], writes=[("ssum", hh)])
                            P.op("dve", lambda e, hh=hh: e.reciprocal(out=rec[hh][0:64, :], in_=ssum[hh][64:128, :]), reads=[("ssum", hh)], writes=[("rec", hh)])
                            P.op("pool", lambda e, hh=hh, tt=tt: e.tensor_tensor(out=OSTh[hh][0:64, tt * TT:(tt + 1) * TT], in0=ssum[hh][0:64, :], in1=rec[hh][0:64, :], op=ALU.mult),
                                 reads=[("ssum", hh), ("rec", hh)], writes=[("OSTh", hh)])

                for (tt, kt) in items:
                    it = (tt, kt, icnt[0])
                    icnt[0] += 1
                    stage_S(it)
                    pend.append(it)
                    if len(pend) >= 2:
                        stage_E(pend[-2])
                    if len(pend) >= 3:
                        stage_V(pend.pop(0))
                stage_E(pend[-1])
                for it in pend:
                    stage_V(it)
                for hh in range(2):
                    P.dma("pool", f"st_os{hh}", self.os_w[hh * 64:(hh + 1) * 64, qc], OSTh[hh][0:64, :].rearrange("p (g t) -> p g t", g=8),
                          reads=[("OSTh", hh)])
            P.barrier()
            P.emit(st)

    def na_phase(self, H2, st):
        P, ps = self.P, self.ps
        wav = self.sb(st, [128, 4096], BF16, "wav")
        for i in range(2):
            P.dma("pool", "w_wav", wav[:, i * 2048:(i + 1) * 2048], self.WAVA[:, i * 2048:(i + 1) * 2048], writes=["wav"], max_dma_last_dim=4096)
        wqk_s = WStream(P, "wqk", [self.sb(st, [128, 2048], BF16, "wqk") for _ in range(2)], [self.WAQK2[j] for j in range(4)])
        wqk_s.get(0)
        VAall = self.sb(st, [128, 8, 32, 128], BF16, "VAall")
        P.op("pool", lambda e: e.memset(VAall[:, :, :, 64:128].rearrange("p h k d -> p (h k) d"), 1.0), writes=["VAones"])
        QA = [self.sb(st, [128, T], BF16, "QA") for _ in range(2)]
        KA = [self.sb(st, [128, T], BF16, "KA") for _ in range(2)]
        for i in range(2):
            P.dma("pool", "ld_oh", KA[i][64:128, :], self.NA_OH, writes=[("KAc", i)], max_dma_last_dim=4096)
            P.dma("pool", "ld_rm", QA[i][64:128, :], self.NA_RM, writes=[("QAc", i)], max_dma_last_dim=4096)
        TH = self.sb(st, [128, 22 * 64], F32, "TH")
        TH8 = [self.sb(st, [128, 22 * 64], BF16, "TH8") for _ in range(2)]
        idb = self.sb(st, [128, 128], BF16, "idb")
        P.op("dve", lambda e: e.tensor_copy(out=idb[:, :], in_=self.ident[:, :]), writes=["idb"])
        OST = [self.sb(st, [128, T], BF16, "OSTn") for _ in range(1)]
        pbufs = [(self.sb(st, [128, 2 * TT], BF16, "pbn"), ("pbn", i)) for i in range(3)]
        rec = self.sb(st, [128, TT], F32, "recn")
        sgroups = [(self.psall[:, 0:1024], ("psg", 0)), (self.psall[:, 1024:2048], ("psg", 1))]
        obks = [(ps[4], ("ps", 4)), (ps[5], ("ps", 5))]
        pring = Ring([(ps[4 + i], ("ps", 4 + i)) for i in range(4)])
        hall = [("H2t", t_) for t_ in range(8)]
        pipe = self.pipe_new(sgroups, pbufs, 2)
        for tk in range(32):
            bank, br = pring.next()
            self.mm_group(bank[:, :], [(H2[:, kc, tk * 128:(tk + 1) * 128], wav[:, kc * 512:(kc + 1) * 512]) for kc in range(8)],
                          reads=["wav", ("H2t", tk // 4)], writes=[br], pre=([], [pring.peek()[1]]))
            dst = VAall[:, :, tk, 0:64]
            src = bank[:, :].rearrange("p (h d) -> p h d", h=8)
            if tk % 2 == 0:
                P.op("act", lambda e, dst=dst, src=src: e.activation(out=dst, in_=src, func=AF.Copy), reads=[br], writes=[("VAn", tk)])
            else:
                P.op("dve", lambda e, dst=dst, src=src: e.tensor_copy(out=dst, in_=src), reads=[br], writes=[("VAn", tk)])
        vres_all = ["VAones"] + [("VAn", tk) for tk in range(32)]
        for j in range(4):
            wslot, wres = wqk_s.get(j)
            for tt in range(8):
                tsl = slice(tt * TT, (tt + 1) * TT)
                hres = [("H2t", tt)]
                bq, bqr = pring.next()
                bk, bkr = pring.next()
                self.mm_group(bq[:, :], [(wslot[:, kc * 128:(kc + 1) * 128], H2[:, kc, tsl]) for kc in range(8)],
                              reads=[wres] + hres, writes=[bqr], pre=([], [bkr]))
                self.mm_group(bk[:, :], [(wslot[:, (8 + kc) * 128:(9 + kc) * 128], H2[:, kc, tsl]) for kc in range(8)],
                              reads=[wres] + hres, writes=[bkr], pre=([], [pring.peek()[1]]))
                for hb_ in range(2):
                    pr = slice(hb_ * 64, hb_ * 64 + 64)
                    P.op("act", lambda e, bq=bq, tsl=tsl, hb_=hb_, pr=pr: e.activation(out=QA[hb_][0:64, tsl], in_=bq[pr, :], func=AF.Copy),
                         reads=[bqr], writes=[("QA", hb_, tt)])
                    P.op("dve", lambda e, bk=bk, tsl=tsl, hb_=hb_, pr=pr: e.tensor_copy(out=KA[hb_][0:64, tsl], in_=bk[pr, :]),
                         reads=[bkr], writes=[("KA", hb_, tt)])
            for hb_ in range(2):
                h = 2 * j + hb_
                P.dma("sp", "ld_th", TH[:, :], self.NA_T[h], writes=["TH"])
                P.op("dve", lambda e, hb_=hb_: e.tensor_scalar(out=TH8[hb_][:, :], in0=TH[:, :], scalar1=8.0, scalar2=None, op0=ALU.mult),
                     reads=["TH"], writes=[("TH8", hb_)])
            for hb_ in range(2):
                h = 2 * j + hb_
                for b in range(8):
                    kts = [4 * b - 2 + jj for jj in range(8) if 0 <= 4 * b - 2 + jj < 32]

                    def expfn(kts_, sg, sres, pb, pres):
                        P.op("act", lambda e: e.activation(out=pb, in_=sg, func=AF.Exp, scale=0.125), reads=[sres], writes=[pres])

                    def extra_fn(kt, b=b, hb_=hb_):
                        jj = kt - (4 * b - 2)
                        e0 = (14 - 2 * jj) * 64
                        return idb[:, :], TH8[hb_][:, e0:e0 + 512], [("TH8", hb_), "idb"]

                    obk = obks[b % 2]

                    def fin(hb_=hb_, b=b, obk=obk):
                        self.finalize(obk[0], obk[1], rec, "recn", OST[0][hb_ * 64:hb_ * 64 + 64, b * TT:(b + 1) * TT], ("OSTn", 0, hb_), hb_ * 64)
                    self.pipe_block(pipe, kts,
                                    lambda kt, hb_=hb_: (KA[hb_][:, kt * 128:(kt + 1) * 128], ("KA", hb_, kt // 4)),
                                    QA[hb_][:, b * TT:(b + 1) * TT], [("QA", hb_, b), ("QAc", hb_), ("KAc", hb_)] + vres_all,
                                    lambda kt, h=h: (VAall[:, h, kt, :], ("VAn", kt)),
                                    obk[0], obk[1], expfn, fin, extra_fn)
            self.pipe_flush(pipe)
            P.dma("sp", "st_osn0", self.os_w[:, j], OST[0][:, :].rearrange("p (g t) -> p g t", g=8),
                  reads=[("OSTn", 0, 0), ("OSTn", 0, 1)])

    def mla_latent(self, H2, st, cqn, ckvn, KB, cosB, sinB):
        P, ps = self.P, self.ps
        P.dma("sp", "ld_tab", cosB[:, :], self.ROPE_B[0], writes=["cosB"])
        P.dma("sp", "ld_tab", sinB[:, :], self.ROPE_B[1], writes=["sinB"])
        wkr = self.sb(st, [128, 2 * 8 * 96], BF16, "wkr")
        P.dma("pool", "w_kr", wkr[:, :], self.WBKR, writes=["wkr"], max_dma_last_dim=4096)
        wcs = [self.sb(st, [128, 1024], BF16, "wcr") for _ in range(5)]
        for i in range(5):
            P.dma("pool", f"w_c{i}", wcs[i][:, :], self.WBC[i], writes=[("wcs", i)], max_dma_last_dim=4096)
        sets = []
        for k in range(2):
            sc = self.common_scratch(st)
            sc["rstd_res"] = ("rstd", k)
            sc["ss"] = (ps[4 * k + 3], ("ps", 4 * k + 3))
            sets.append(dict(sc=sc, pj=[(ps[4 * k + i], ("ps", 4 * k + i)) for i in range(3)],
                             t1=self.sb(st, [128, TT], F32, "t1m"), t2=self.sb(st, [128, TT], F32, "t2m")))
        for tt in range(8):
            tsl = slice(tt * TT, (tt + 1) * TT)
            hres = [("H2t", tt)]
            S = sets[tt % 2]
            pj, sc = S["pj"], S["sc"]
            for (c0, nch, dst, dname, gc0, dim) in ((0, 3, cqn, "cqn", 0, 384.0), (3, 2, ckvn, "ckvn", 3, 256.0)):
                for c in range(nch):
                    self.mm_group(pj[c][0][:, :], [(wcs[c0 + c][:, kc * 128:(kc + 1) * 128], H2[:, kc, tsl]) for kc in range(8)],
                                  reads=[("wcs", c0 + c)] + hres, writes=[pj[c][1]])
                self.norm_tile([pj[c][0][:, :] for c in range(nch)], [pj[c][1] for c in range(nch)],
                               [self.abg[:, gc0 + c:gc0 + c + 1] for c in range(nch)],
                               [dst[:, c, tsl] for c in range(nch)], [(dname, c, tt) for c in range(nch)],
                               dim, self.ones_bf[:, :], sc)
            self.mm_group(pj[0][0][0:96, :], [(wkr[:, kc * 96:(kc + 1) * 96], H2[:, kc, tsl]) for kc in range(8)],
                          reads=["wkr"] + hres, writes=[pj[0][1]])
            self.mm_group(pj[1][0][0:96, :], [(wkr[:, (8 + kc) * 96:(9 + kc) * 96], H2[:, kc, tsl]) for kc in range(8)],
                          reads=["wkr"] + hres, writes=[pj[1][1]])
            self.rope32(pj[0], pj[1], cosB, sinB, tsl, S["t1"], S["t2"], [KB[0][64:96, tsl], KB[1][64:96, tsl]],
                        [("KBr", 0, tt), ("KBr", 1, tt)], tag=tt % 2)

    def mla_attn(self, st, cqn, ckvn, KB, cosB, sinB):
        P, ps = self.P, self.ps
        wuq = self.sb(st, [128, 8 * 576], BF16, "wuq")
        for i in range(3):
            P.dma("pool", "w_uq", wuq[:, i * 1536:(i + 1) * 1536], self.WBUQ[:, i * 1536:(i + 1) * 1536], writes=["wuq"], max_dma_last_dim=4096)
        wukn = self.sb(st, [128, 1024], BF16, "wukn")
        P.dma("pool", "w_ukn", wukn[:, :], self.WBUKN, writes=["wukn"], max_dma_last_dim=4096)
        wuv = self.sb(st, [128, 1024], BF16, "wuv")
        P.dma("pool", "w_uv", wuv[:, :], self.WBUV, writes=["wuv"], max_dma_last_dim=4096)
        QB = [self.sb(st, [128, T], BF16, "QB") for _ in range(2)]
        VA = [self.sb(st, [128, 32, 128], BF16, "VAm") for _ in range(2)]
        VAx = [self.sb(st, [128, 32, 128], BF16, "VAmx") for _ in range(2)]
        for i in range(2):
            P.op("pool", lambda e, i=i: e.memset(VA[i][:, :, 64:128], 1.0), writes=[("VAm", i)])
        OST = [self.sb(st, [128, T], BF16, "OSTm") for _ in range(2)]
        pbufs = [(self.sb(st, [128, 2 * TT], BF16, "pbm"), ("pbm", i)) for i in range(3)]
        rec = self.sb(st, [128, TT], F32, "recm")
        t1 = self.sb(st, [128, TT], F32, "t1m")
        t2 = self.sb(st, [128, TT], F32, "t2m")
        cres = [("cqn", c, tt) for c in range(3) for tt in range(8)]
        kvres = [("ckvn", c, tt) for c in range(2) for tt in range(8)]
        sgroups = [(self.psall[:, 0:1024], ("psg", 0)), (self.psall[:, 1024:2048], ("psg", 1))]
        obks = [(ps[4], ("ps", 4)), (ps[5], ("ps", 5))]
        msets = [((ps[5], ("ps", 5)), (ps[6], ("ps", 6)), (ps[7], ("ps", 7))),
                 ((ps[0], ("psg", 0)), (ps[1], ("psg", 0)), (ps[2], ("psg", 1)))]
        t1s = [t1, self.sb(st, [128, TT], F32, "t1mb")]
        t2s = [t2, self.sb(st, [128, TT], F32, "t2mb")]
        scale = 96.0 ** -0.5
        pipe = self.pipe_new(sgroups, pbufs, 2)
        for h in range(8):
            hb_ = h % 2
            for tt in range(8):
                tsl = slice(tt * TT, (tt + 1) * TT)
                o0 = h * 576
                k_ = tt % 2
                pa, pb_, bkk = msets[k_]
                self.mm_group(pa[0][0:96, :], [(wuq[:, o0 + kc * 96:o0 + (kc + 1) * 96], cqn[:, kc, tsl]) for kc in range(3)],
                              reads=["wuq"] + [("cqn", c, tt) for c in range(3)], writes=[pa[1]])
                self.mm_group(pb_[0][0:96, :], [(wuq[:, o0 + 288 + kc * 96:o0 + 288 + (kc + 1) * 96], cqn[:, kc, tsl]) for kc in range(3)],
                              reads=["wuq"] + [("cqn", c, tt) for c in range(3)], writes=[pb_[1]])
                P.op("act", lambda e, tsl=tsl, hb_=hb_, pa=pa: e.activation(out=QB[hb_][0:64, tsl], in_=pa[0][0:64, :], func=AF.Copy),
                     reads=[pa[1]], writes=[("QB", hb_, tt)])
                self.rope32(pa, pb_, cosB, sinB, tsl, t1s[k_], t2s[k_], [QB[hb_][64:96, tsl]], [("QBr", hb_, tt)], tag=k_)
                bk, bkr = bkk
                self.mm_group(bk[0:64, :], [(wukn[:, (h * 2 + kc) * 64:(h * 2 + kc + 1) * 64], ckvn[:, kc, tsl]) for kc in range(2)],
                              reads=["wukn"] + [("ckvn", c, tt) for c in range(2)], writes=[bkr])
                P.op("dve", lambda e, bk=bk, tsl=tsl, hb_=hb_: e.tensor_copy(out=KB[hb_][0:64, tsl], in_=bk[0:64, :]),
                     reads=[bkr], writes=[("KB", hb_, tt)])
            self.vproj(None, 2, 64, lambda kc, tk: ckvn[:, kc, tk * 128:(tk + 1) * 128], kvres,
                       self._wuv_head(wuv, h), "wuv",
                       lambda: [(VA[hb_], ("VAm", hb_), 0)], Ring([(ps[5], ("ps", 5)), (ps[6], ("ps", 6))]))
            P.op("dve", lambda e, hb_=hb_: e.tensor_scalar(out=VAx[hb_][:, :, :], in0=VA[hb_][:, :, :], scalar1=self.maskT[:, 4:5], scalar2=None, op0=ALU.mult),
                 reads=[("VAm", hb_)], writes=[("VAmx", hb_)])
            for qb in range(8):
                qseg = qb // 4

                def expfn(kts, sg, sres, pb, pres):
                    P.op("act", lambda e: e.activation(out=pb, in_=sg, func=AF.Exp, scale=scale), reads=[sres], writes=[pres])

                def vfn(kt, hb_=hb_, qseg=qseg):
                    if kt // 16 != qseg:
                        return VAx[hb_][:, kt, :], ("VAmx", hb_)
                    return VA[hb_][:, kt, :], ("VAm", hb_)
                oi = (h // 2) % 2

                obk = obks[qb % 2]

                def fin(hb_=hb_, qb=qb, oi=oi, obk=obk):
                    self.finalize(obk[0], obk[1], rec, "recm", OST[oi][hb_ * 64:hb_ * 64 + 64, qb * TT:(qb + 1) * TT], ("OSTm", oi, hb_), hb_ * 64)
                self.pipe_block(pipe, list(range(32)),
                                lambda kt, hb_=hb_: (KB[hb_][0:96, kt * 128:(kt + 1) * 128], ("KB", hb_, kt // 4)),
                                QB[hb_][0:96, qb * TT:(qb + 1) * TT],
                                [("QB", hb_, qb), ("QBr", hb_, qb)] + [("KBr", hb_, t_) for t_ in range(8)],
                                vfn, obk[0], obk[1], expfn, fin)
            self.pipe_flush(pipe)
            if hb_ == 1:
                oi = (h // 2) % 2
                P.dma("sp", f"st_osm{oi}", self.os_w[:, 4 + h // 2], OST[oi][:, :].rearrange("p (g t) -> p g t", g=8),
                      reads=[("OSTm", oi, 0), ("OSTm", oi, 1)])

    def _wuv_head(self, wuv, h):
        return wuv[:, h * 128:(h + 1) * 128]

    def rope32(self, praw, pswp, cosB, sinB, tsl, t1, t2, dsts, dres, tag=0):
        P = self.P
        pr = slice(64, 96)
        n1, n2 = ("t1m", tag), ("t2m", tag)
        P.op("dve", lambda e: e.tensor_tensor(out=t1[pr, :], in0=praw[0][pr, :], in1=cosB[pr, tsl], op=ALU.mult),
             reads=[praw[1], "cosB"], writes=[n1])
        P.op("dve", lambda e: e.tensor_tensor(out=t2[pr, :], in0=pswp[0][pr, :], in1=sinB[pr, tsl], op=ALU.mult),
             reads=[pswp[1], "sinB"], writes=[n2])
        for d, r in zip(dsts, dres):
            P.op("pool", lambda e, d=d: e.tensor_tensor(out=d, in0=t1[pr, :], in1=t2[pr, :], op=ALU.add),
                 reads=[n1, n2], writes=[r])

    def mixer0(self):
        P = self.P
        with ExitStack() as st0:
            cqn = self.sb(st0, [128, 3, T], BF16, "cqn")
            ckvn = self.sb(st0, [128, 2, T], BF16, "ckvn")
            KB = [self.sb(st0, [128, T], BF16, "KB") for _ in range(2)]
            cosB = self.sb(st0, [128, T], F32, "cosB")
            sinB = self.sb(st0, [128, T], F32, "sinB")
            with ExitStack() as st:
                H2 = self.load_H2(st)
                self.mla_latent(H2, st, cqn, ckvn, KB, cosB, sinB)
                P.barrier()
                P.emit(st)
            with ExitStack() as st2:
                self.mla_attn(st2, cqn, ckvn, KB, cosB, sinB)
                P.barrier()
                P.emit(st2)
        with ExitStack() as st:
            H2 = self.load_H2(st)
            self.na_phase(H2, st)
            P.barrier()
            P.emit(st)

    def build(self):
        nc = self.nc
        dbg = self.debug
        self.X = self.din("x", [T, 1024])
        self.Y = nc.dram_tensor("y", [T, 1024], F32, kind="ExternalOutput").ap()
        self.WGU = self.din("wgu", [4, NFC, 128, 2048])
        self.WD = self.din("wd", [4, NFC, 128, 1024])
        self.WO = self.din("wo", [2, 8, 128, 1024])
        self.NGd = self.din("ng", [128, 56])
        self.IDd = self.din("ident", [128, 128])
        self.ONd = self.din("onesblk", [2, 128, 128])
        self.MKd = self.din("maskt", [128, 8])
        self.HGd = self.din("hg", [128, 4])
        self.ABGd = self.din("abg", [128, 5])
        self.ROPE_C = self.din("rope_c", [2, 128, T])
        self.ROPE_B = self.din("rope_b", [2, 128, T])
        self.WCQ = self.din("wcq", [8, 128, 2048])
        self.WCK = self.din("wck", [2, 128, 2048])
        self.WCV = self.din("wcv", [128, 2048])
        self.WAQK2 = self.din("waqk2", [4, 128, 2048])
        self.WAVA = self.din("wava", [128, 4096])
        self.NA_OH = self.din("na_oh", [64, T])
        self.NA_RM = self.din("na_rm", [64, T])
        self.NA_T = self.din("na_t", [8, 128, 22 * 64])
        self.WBC = self.din("wbc", [5, 128, 1024])
        self.WBKR = self.din("wbkr", [128, 1536])
        self.WBUQ = self.din("wbuq", [128, 4608])
        self.WBUKN = self.din("wbukn", [128, 1024])
        self.WBUV = self.din("wbuv", [128, 1024])
        kind = "ExternalOutput" if dbg else "Internal"
        self.xs_tm = nc.dram_tensor("xs", [8, 128, 8, TT], F32, kind=kind).ap()
        self.hs_tm = nc.dram_tensor("hs", [8, 128, 8, TT], BF16, kind=kind).ap()
        self.os_tm = nc.dram_tensor("os", [8, 128, 8, TT], BF16, kind=kind).ap()
        self.os_w = self.os_tm.rearrange("g p c t -> p c g t")
        with ExitStack() as gst:
            P = self.P = Prog(nc, gst)
            self.psall = gst.enter_context(nc.psum_tensor("psall", [128, 4096], F32))
            self.ps = [self.psall[:, i * 512:(i + 1) * 512] for i in range(8)]
            self.ng = self.sb(gst, [128, 56], F32, "ng")
            self.ident = self.sb(gst, [128, 128], F32, "ident")
            self.ones_bf = self.sb(gst, [128, 128], BF16, "ones")
            self.blk_bf = self.sb(gst, [128, 128], BF16, "blk")
            self.maskT = self.sb(gst, [128, 8], F32, "maskT")
            self.hg = self.sb(gst, [128, 4], F32, "hg")
            self.abg = self.sb(gst, [128, 5], F32, "abg")
            self.epsc = self.sb(gst, [128, 1], F32, "epsc")
            P.dma("sp", "ld_c", self.ng[:, :], self.NGd, writes=["c"])
            P.dma("sp", "ld_c", self.ident[:, :], self.IDd, writes=["c"])
            P.dma("sp", "ld_c", self.maskT[:, :], self.MKd, writes=["c"])
            P.dma("sp", "ld_c", self.hg[:, :], self.HGd, writes=["c"])
            P.dma("sp", "ld_c", self.abg[:, :], self.ABGd, writes=["c"])
            P.dma("pool", "ld_c2", self.ones_bf[:, :], self.ONd[0], writes=["c2"])
            P.dma("pool", "ld_c2", self.blk_bf[:, :], self.ONd[1], writes=["c2"])
            P.op("dve", lambda e: e.memset(self.epsc[:, :], EPS), writes=["c3"])
            P.barrier()
            stages = [lambda: self.front(0), self.mixer0, lambda: self.front(1), self.gqa_phase, lambda: self.front(2)]
            n = len(stages) if self.stop_after is None else self.stop_after
            for sfn in stages[:n]:
                sfn()
            with ExitStack() as st:
                P.barrier()
                P.emit(st)
        return nc


def _chunk_k(w, m):
    K, M = w.shape
    return np.ascontiguousarray(w.reshape(K // 128, 128, M).transpose(1, 0, 2).reshape(128, (K // 128) * M))


def _axial_angles(pos, rot_dim):
    row = (pos // 64).astype(np.float32)
    col = (pos % 64).astype(np.float32)
    axis_dim = rot_dim // 2
    inv = (np.float32(10000.0) ** (-np.arange(0, axis_dim, 2, dtype=np.float32) / np.float32(axis_dim))).astype(np.float32)
    return np.concatenate([row[:, None] * inv, col[:, None] * inv], axis=-1).astype(np.float32)


def _core_tables(is_prompt):
    u = np.arange(T)
    pos = u if is_prompt else (u % SEG)
    ang = _axial_angles(pos, 64)
    d = np.arange(128) % 64
    cosC = np.cos(ang)[:, d % 32].T.astype(np.float32)
    sinC = (np.sin(ang)[:, d % 32] * np.where(d < 32, -1.0, 1.0)[None, :]).T.astype(np.float32)
    rope_c = np.stack([cosC, sinC]).astype(np.float32)
    angb = _axial_angles(pos, 32)
    rope_b = np.zeros((2, 128, T), np.float32)
    dd = np.arange(32)
    rope_b[0, 64:96] = np.cos(angb)[:, dd % 16].T
    rope_b[1, 64:96] = (np.sin(angb)[:, dd % 16] * np.where(dd < 16, -1.0, 1.0)[None, :]).T
    maskt = np.zeros((128, 8), np.float32)
    maskt[:, 4] = 1.0
    if not is_prompt:
        maskt[:, 1] = NEG
        maskt[:, 2] = NEG
        maskt[:, 4] = 0.0
    lrow = u // 64
    oh = (np.arange(64)[:, None] == lrow[None, :]).astype(np.float32)
    if is_prompt:
        rs = np.clip(lrow - 4, 0, 56)
    else:
        base = (lrow // 32) * 32
        rs = np.clip((lrow - base) - 4, 0, 24) + base
    kr = np.arange(64)[:, None]
    rm = np.where((kr >= rs[None, :]) & (kr < rs[None, :] + 8), 0.0, NEG).astype(np.float32)
    return dict(rope_c=rope_c, rope_b=rope_b, maskt=maskt, na_oh=oh, na_rm=rm)


def _na_bias_table(bias):
    H = bias.shape[0]
    i = np.arange(2)[:, None, None, None]
    kc = np.arange(64)[None, :, None, None]
    e = np.arange(22)[None, None, :, None]
    qc = np.arange(64)[None, None, None, :]
    dr = 17 - e + i
    dc = kc - qc + 15
    cs = np.clip(qc - 8, 0, 48)
    valid = (dr >= 0) & (dr <= 14) & (kc >= cs) & (kc < cs + 16)
    valid = np.broadcast_to(valid, (2, 64, 22, 64))
    drc = np.broadcast_to(np.clip(dr, 0, 14), (2, 64, 22, 64))
    dcc = np.broadcast_to(np.clip(dc, 0, 30), (2, 64, 22, 64))
    out = np.empty((H, 2, 64, 22, 64), np.float32)
    for h in range(H):
        out[h] = np.where(valid, bias[h][drc, dcc], np.float32(NEG))
    return np.ascontiguousarray(out.reshape(H, 128, 22 * 64))


def _prep_shared(inp):
    f = lambda a: np.asarray(a, dtype=np.float32)
    sh = {}
    wg = [f(inp["ffn1_wg"])[0], f(inp["ffn2_wg"])[0], f(inp["ffn1_wg"])[1], f(inp["ffn2_wg"])[1]]
    wu = [f(inp["ffn1_wu"])[0], f(inp["ffn2_wu"])[0], f(inp["ffn1_wu"])[1], f(inp["ffn2_wu"])[1]]
    wd = [f(inp["ffn1_wd"])[0], f(inp["ffn2_wd"])[0], f(inp["ffn1_wd"])[1], f(inp["ffn2_wd"])[1]]
    FP = NFC * 128
    WGU = np.zeros((4, NFC, 128, 2, 8, 128), np.float32)
    WD = np.zeros((4, NFC, 128, 1024), np.float32)
    for i in range(4):
        g = np.zeros((1024, FP), np.float32); g[:, :2752] = wg[i]
        u_ = np.zeros((1024, FP), np.float32); u_[:, :2752] = wu[i]
        d_ = np.zeros((FP, 1024), np.float32); d_[:2752] = wd[i]
        WGU[i, :, :, 0] = g.reshape(8, 128, NFC, 128).transpose(2, 1, 0, 3)
        WGU[i, :, :, 1] = u_.reshape(8, 128, NFC, 128).transpose(2, 1, 0, 3)
        WD[i] = d_.reshape(NFC, 128, 1024)
    sh["wgu"] = WGU.reshape(4, NFC, 128, 2048)
    sh["wd"] = WD
    gl = [f(inp["norm_ffn1"])[0], f(inp["norm_mix"])[0], f(inp["norm_ffn2"])[0],
          f(inp["norm_ffn1"])[1], f(inp["norm_mix"])[1], f(inp["norm_ffn2"])[1], f(inp["final_norm"])]
    sh["ng"] = np.ascontiguousarray(np.concatenate([g.reshape(8, 128).T for g in gl], axis=1))
    sh["ident"] = np.eye(128, dtype=np.float32)
    blk = np.zeros((128, 128), np.float32); blk[:64, :64] = 1.0; blk[64:, 64:] = 1.0
    sh["onesblk"] = np.stack([np.ones((128, 128), np.float32), blk])
    w_in = f(inp["ab_w_in"])[0]
    qa, ka, va = w_in[:, 0:512], w_in[:, 512:1024], w_in[:, 1024:1536]
    cq, ckv, kr = w_in[:, 1536:1920], w_in[:, 1920:2176], w_in[:, 2176:2208]
    waqk2 = np.zeros((4, 128, 2, 8, 128), np.float32)
    for j in range(4):
        waqk2[j, :, 0] = qa[:, j * 128:(j + 1) * 128].reshape(8, 128, 128).transpose(1, 0, 2)
        waqk2[j, :, 1] = ka[:, j * 128:(j + 1) * 128].reshape(8, 128, 128).transpose(1, 0, 2)
    sh["waqk2"] = waqk2.reshape(4, 128, 2048)
    sh["wava"] = _chunk_k(va, 512)
    sh["na_t"] = _na_bias_table(f(inp["ab_na_bias"])[0])
    wbc = [_chunk_k(cq[:, c * 128:(c + 1) * 128], 128) for c in range(3)] + [_chunk_k(ckv[:, c * 128:(c + 1) * 128], 128) for c in range(2)]
    sh["wbc"] = np.stack(wbc)
    sw16 = (np.arange(32) + 16) % 32
    krp = np.zeros((2, 1024, 96), np.float32)
    krp[0, :, 64:] = kr
    krp[1, :, 64:] = kr[:, sw16]
    sh["wbkr"] = np.concatenate([_chunk_k(krp[0], 96), _chunk_k(krp[1], 96)], axis=1)
    wuq = f(inp["ab_w_uq"])[0]
    parts = []
    for h in range(8):
        wh = wuq[:, h * 96:(h + 1) * 96]
        whs = wh.copy()
        whs[:, 64:] = wh[:, 64:][:, sw16]
        parts += [_chunk_k(wh, 96), _chunk_k(whs, 96)]
    sh["wbuq"] = np.concatenate(parts, axis=1)
    wukv = f(inp["ab_w_ukv"])[0]
    sh["wbukn"] = np.concatenate([_chunk_k(wukv[:, h * 128:h * 128 + 64], 64) for h in range(8)], axis=1)
    sh["wbuv"] = np.concatenate([_chunk_k(wukv[:, h * 128 + 64:h * 128 + 128], 64) for h in range(8)], axis=1)
    qn, kvn = f(inp["ab_q_norm"])[0], f(inp["ab_kv_norm"])[0]
    sh["abg"] = np.ascontiguousarray(np.concatenate([qn.reshape(3, 128).T, kvn.reshape(2, 128).T], axis=1))
    wo0 = f(inp["ab_w_out"])[0]
    cw = f(inp["c_w_in"])[0]
    qw, kw, vw = cw[:, :1024], cw[:, 1024:1280], cw[:, 1280:1536]
    sw32 = (np.arange(64) + 32) % 64
    heads = []
    for qc in range(8):
        if qc < 4:
            heads.append((qc, 4 + qc))
        else:
            heads.append((8 + qc - 4, 12 + qc - 4))
    wcq = np.zeros((8, 128, 2, 8, 128), np.float32)
    rowidx = np.zeros((8, 128), np.int64)
    for qc, (ha, hb) in enumerate(heads):
        cols = np.concatenate([np.arange(ha * 64, ha * 64 + 64), np.arange(hb * 64, hb * 64 + 64)])
        colsw = np.concatenate([ha * 64 + sw32, hb * 64 + sw32])
        wcq[qc, :, 0] = qw[:, cols].reshape(8, 128, 128).transpose(1, 0, 2)
        wcq[qc, :, 1] = qw[:, colsw].reshape(8, 128, 128).transpose(1, 0, 2)
        rowidx[qc] = cols
    sh["wcq"] = wcq.reshape(8, 128, 2048)
    wck = np.zeros((2, 128, 2, 8, 128), np.float32)
    for kch in range(2):
        cols = np.arange(kch * 128, kch * 128 + 128)
        colsw = np.concatenate([kch * 128 + sw32, kch * 128 + 64 + sw32])
        wck[kch, :, 0] = kw[:, cols].reshape(8, 128, 128).transpose(1, 0, 2)
        wck[kch, :, 1] = kw[:, colsw].reshape(8, 128, 128).transpose(1, 0, 2)
    sh["wck"] = wck.reshape(2, 128, 2048)
    sh["wcv"] = _chunk_k(vw, 256)
    gq, gk = f(inp["c_q_norm"])[0], f(inp["c_k_norm"])[0]
    d = np.arange(128) % 64
    sh["hg"] = np.ascontiguousarray(np.stack([gq[d], gq[sw32[d]], gk[d], gk[sw32[d]]], axis=1))
    wo1 = f(inp["c_w_out"])[0][rowidx.reshape(-1)]
    WO = np.zeros((2, 8, 128, 8, 128), np.float32)
    for li, wo in enumerate((wo0, wo1)):
        WO[li] = wo.reshape(8, 128, 8, 128).transpose(2, 1, 0, 3)
    sh["wo"] = WO.reshape(2, 8, 128, 1024)
    return sh


_NC_CACHE = {}


def _get_nc(debug=False, stop_after=None):
    key = (debug, stop_after)
    if key not in _NC_CACHE:
        _NC_CACHE[key] = Builder(debug=debug, stop_after=stop_after).build()
    return _NC_CACHE[key]


def make_in_maps(inputs):
    sh = _prep_shared(inputs)
    xp = np.asarray(inputs["x_prompt"], dtype=np.float32)
    xsm = np.asarray(inputs["x_sample"], dtype=np.float32)
    tp, ts = _core_tables(True), _core_tables(False)
    maps = []
    for c in range(N_CORES):
        m = dict(sh)
        if c < 4:
            m["x"] = np.ascontiguousarray(xp[c])
            m.update(tp)
        else:
            m["x"] = np.ascontiguousarray(xsm[2 * (c - 4):2 * (c - 4) + 2].reshape(T, 1024))
            m.update(ts)
        maps.append(m)
    return maps


def kernel(**inputs):
    nc = _get_nc()
    maps = make_in_maps(inputs)
    res = run_bass_kernel_spmd(nc, maps, core_ids=list(range(N_CORES)))
    ys = [np.asarray(r["y"], dtype=np.float32) for r in res.results]
    y_prompt = np.stack(ys[:4]).reshape(4, 4096, 1024)
    y_sample = np.stack([y.reshape(2, 2048, 1024) for y in ys[4:]]).reshape(8, 2048, 1024)
    return (y_prompt, y_sample)
```
